# Optimizing a Trainium2 kernel written in Bass

```python
import math
import jax, jax.numpy as jnp
from jax import lax
import numpy as np

D_MODEL = 2048
BATCH = 16
SEQ = 2048
DEPTH = 1
DEC_BATCH = 8
DEC_SEQ = 64
PAST_LEN = 1024

CHUNK = 64
Q_BLOCK = 128
N_MEM = 256
EPS = 1e-6
SSD_D_INNER = D_MODEL
SSD_HEAD_DIM = 64
SSD_HEADS = SSD_D_INNER // SSD_HEAD_DIM
SSD_GROUPS = 4
SSD_D_STATE = 128
SSD_CONV = 4
SSD_CONV_CH = SSD_D_INNER + 2 * SSD_GROUPS * SSD_D_STATE
FOX_HEADS = 16
FOX_HEAD_DIM = 128
FOX_WIDTH = FOX_HEADS * FOX_HEAD_DIM
FORGET_BIAS_INIT = 3.0
MEM_HEADS = 4
MEM_HEAD_DIM = 512
MEM_WIDTH = MEM_HEADS * MEM_HEAD_DIM
SIZES = [SSD_D_INNER, SSD_CONV_CH, SSD_HEADS, FOX_WIDTH, FOX_WIDTH, FOX_WIDTH, FOX_WIDTH, FOX_HEADS,
         MEM_WIDTH, MEM_WIDTH, D_MODEL, D_MODEL, D_MODEL]
IN_WIDTH = sum(SIZES)
SPLITS = [int(v) for v in np.cumsum(SIZES)[:-1]]

kernel_name = 'hybrid_ssd_fox_mem_stream_step'


def rmsnorm(x, g):
    xf = x.astype(jnp.float32)
    y = xf * lax.rsqrt(jnp.mean(xf * xf, axis=-1, keepdims=True) + EPS)
    return (y * g.astype(jnp.float32)).astype(x.dtype)


def causal_conv(x_pad, w, b):
    T = x_pad.shape[1] - (SSD_CONV - 1)
    out = b
    for i in range(SSD_CONV):
        out = out + x_pad[:, i:i + T] * w[i]
    return out


def ssd_scan(x, dt, A, Bm, Cm, h0, chunk):
    f32 = jnp.float32
    b, T, H, P = x.shape
    G, N = Bm.shape[2], Bm.shape[3]
    J = H // G
    nc = T // chunk
    xc = x.astype(f32).reshape(b, nc, chunk, G, J, P)
    Bc = Bm.astype(f32).reshape(b, nc, chunk, G, N)
    Cc = Cm.astype(f32).reshape(b, nc, chunk, G, N)
    dtc = dt.reshape(b, nc, chunk, G, J)
    acum = jnp.cumsum(dtc * A.reshape(G, J), axis=2)
    tri = (jnp.arange(chunk)[:, None] >= jnp.arange(chunk)[None, :])[:, :, None, None]
    diff = acum[:, :, :, None] - acum[:, :, None]
    ldec = jnp.exp(jnp.where(tri, diff, -jnp.inf))
    cb = jnp.einsum('bclgn,bcsgn->bclsg', Cc, Bc)
    m = cb[..., None] * ldec * dtc[:, :, None]
    y_diag = jnp.einsum('bclsgj,bcsgjp->bclgjp', m, xc)
    w_end = jnp.exp(acum[:, :, -1:] - acum) * dtc
    states = jnp.einsum('bcsgn,bcsgjp->bcgjpn', Bc, xc * w_end[..., None])
    chunk_dec = jnp.exp(acum[:, :, -1])

    def step(h, inp):
        dec, st = inp
        return h * dec[..., None, None] + st, h

    h_init = h0.astype(f32).reshape(b, G, J, P, N)
    h_final, h_in = lax.scan(step, h_init, (jnp.moveaxis(chunk_dec, 1, 0), jnp.moveaxis(states, 1, 0)))
    h_in = jnp.moveaxis(h_in, 0, 1)
    y_off = jnp.einsum('bclgn,bcgjpn->bclgjp', Cc, h_in) * jnp.exp(acum)[..., None]
    y = (y_diag + y_off).reshape(b, T, H, P)
    return y, h_final.reshape(b, H, P, N)


def fox_attend(q, k, v, cq, ck, q_pos, k_pos):
    s = jnp.einsum('bqhd,bkhd->bhqk', q, k).astype(jnp.float32) * (FOX_HEAD_DIM ** -0.5)
    s = s + jnp.transpose(cq, (0, 2, 1))[..., None] - jnp.transpose(ck, (0, 2, 1))[:, :, None, :]
    mask = k_pos[None, :] <= q_pos[:, None]
    p = jax.nn.softmax(jnp.where(mask, s, -jnp.inf), axis=-1).astype(v.dtype)
    return jnp.einsum('bhqk,bkhd->bqhd', p, v)


def mem_kv(mem, g_mem, w_mem_kv):
    b = mem.shape[0]
    kv = rmsnorm(mem, g_mem) @ w_mem_kv
    k, v = jnp.split(kv, [MEM_WIDTH], axis=-1)
    return (k.reshape(b, N_MEM, MEM_HEADS, MEM_HEAD_DIM), v.reshape(b, N_MEM, MEM_HEADS, MEM_HEAD_DIM))


def mem_attend(q, k, v):
    s = jnp.einsum('bqhd,bkhd->bhqk', q, k).astype(jnp.float32) * (MEM_HEAD_DIM ** -0.5)
    p = jax.nn.softmax(s, axis=-1).astype(v.dtype)
    return jnp.einsum('bhqk,bkhd->bqhd', p, v)


def mixer_layer(x, conv_buf, h0, k_past, v_past, logf_past, mem_k, mem_v,
                g_norm, w_in, w_conv, b_conv, dt_bias, a_log, d_skip, g_ssd_out, b_forget,
                w_o_ssd, w_o_fox, w_o_mem, w_out, ssd_chunk, q_block):
    f32 = jnp.float32
    b, T, _ = x.shape
    P = k_past.shape[1]
    h = rmsnorm(x, g_norm)
    u = h @ w_in
    z, xbc, dt_raw, fq, fk, fv, fg, ff, mq, mg, gs, gf, gm = jnp.split(u, SPLITS, axis=-1)

    xbc_pad = jnp.concatenate([conv_buf.astype(xbc.dtype), xbc], axis=1)
    new_conv = xbc_pad[:, -(SSD_CONV - 1):]
    xbc_c = jax.nn.silu(causal_conv(xbc_pad, w_conv, b_conv))
    xs, Bm, Cm = jnp.split(xbc_c, [SSD_D_INNER, SSD_D_INNER + SSD_GROUPS * SSD_D_STATE], axis=-1)
    dt = jax.nn.softplus(dt_raw.astype(f32) + dt_bias.astype(f32))
    A = -jnp.exp(a_log.astype(f32))
    xh = xs.reshape(b, T, SSD_HEADS, SSD_HEAD_DIM)
    y, h_final = ssd_scan(xh, dt, A, Bm.reshape(b, T, SSD_GROUPS, SSD_D_STATE),
                          Cm.reshape(b, T, SSD_GROUPS, SSD_D_STATE), h0, ssd_chunk)
    y = (y + d_skip.astype(f32)[:, None] * xh.astype(f32)).reshape(b, T, SSD_D_INNER).astype(x.dtype)
    y_ssd = rmsnorm(y * jax.nn.silu(z), g_ssd_out)

    q = fq.reshape(b, T, FOX_HEADS, FOX_HEAD_DIM)
    k_new = fk.reshape(b, T, FOX_HEADS, FOX_HEAD_DIM)
    v_new = fv.reshape(b, T, FOX_HEADS, FOX_HEAD_DIM)
    logf_new = jax.nn.log_sigmoid(ff.astype(f32) + b_forget.astype(f32))
    k_all = jnp.concatenate([k_past.astype(k_new.dtype), k_new], axis=1)
    v_all = jnp.concatenate([v_past.astype(v_new.dtype), v_new], axis=1)
    c_all = jnp.cumsum(jnp.concatenate([logf_past.astype(f32), logf_new], axis=1), axis=1)
    k_pos = jnp.arange(P + T)
    q_pos = P + jnp.arange(T)
    cq = c_all[:, P:]
    if q_block is None:
        o = fox_attend(q, k_all, v_all, cq, c_all, q_pos, k_pos)
    else:
        nb = T // q_block
        qb = q.reshape(b, nb, q_block, FOX_HEADS, FOX_HEAD_DIM).transpose(1, 0, 2, 3, 4)
        cqb = cq.reshape(b, nb, q_block, FOX_HEADS).transpose(1, 0, 2, 3)
        pb = q_pos.reshape(nb, q_block)
        o = lax.map(lambda a: fox_attend(a[0], k_all, v_all, a[1], c_all, a[2], k_pos), (qb, cqb, pb))
        o = o.transpose(1, 0, 2, 3, 4)
    y_fox = o.reshape(b, T, FOX_WIDTH) * jax.nn.silu(fg)

    om = mem_attend(mq.reshape(b, T, MEM_HEADS, MEM_HEAD_DIM), mem_k.astype(mq.dtype), mem_v.astype(mq.dtype))
    y_mem = om.reshape(b, T, MEM_WIDTH) * jax.nn.silu(mg)

    merged = (jax.nn.sigmoid(gs) * (y_ssd @ w_o_ssd) + jax.nn.sigmoid(gf) * (y_fox @ w_o_fox)
              + jax.nn.sigmoid(gm) * (y_mem @ w_o_mem))
    x_out = x + merged @ w_out
    return x_out, new_conv, h_final, k_new, v_new, logf_new


def setup_inputs(seed: int = 0) -> dict:
    key = jax.random.key(seed)
    ks = jax.random.split(key, 32)
    f32 = jnp.float32

    def nrm(k, shape, scale):
        return jax.random.normal(k, shape, f32) * scale

    L = DEPTH
    dt0 = jnp.exp(jax.random.uniform(ks[14], (L, SSD_HEADS), f32, math.log(1e-3), math.log(1e-1)))
    return {
        'x_prompt': nrm(ks[0], (BATCH, SEQ, D_MODEL), 1.0),
        'x_sample': nrm(ks[1], (DEC_BATCH, DEC_SEQ, D_MODEL), 1.0),
        'mem_prompt': nrm(ks[2], (BATCH, N_MEM, D_MODEL), 1.0),
        'cache_fox_k': nrm(ks[3], (L, DEC_BATCH, PAST_LEN, FOX_HEADS, FOX_HEAD_DIM), 1.0),
        'cache_fox_v': nrm(ks[4], (L, DEC_BATCH, PAST_LEN, FOX_HEADS, FOX_HEAD_DIM), 1.0),
        'cache_fox_logf': jax.nn.log_sigmoid(FORGET_BIAS_INIT + nrm(ks[5], (L, DEC_BATCH, PAST_LEN, FOX_HEADS), 1.0)),
        'state_ssd': nrm(ks[6], (L, DEC_BATCH, SSD_HEADS, SSD_HEAD_DIM, SSD_D_STATE), 0.1),
        'state_ssd_conv': nrm(ks[7], (L, DEC_BATCH, SSD_CONV - 1, SSD_CONV_CH), 1.0),
        'cache_mem_k': nrm(ks[8], (L, DEC_BATCH, N_MEM, MEM_HEADS, MEM_HEAD_DIM), 1.0),
        'cache_mem_v': nrm(ks[9], (L, DEC_BATCH, N_MEM, MEM_HEADS, MEM_HEAD_DIM), 1.0),
        'g_norm': 1.0 + nrm(ks[10], (L, D_MODEL), 0.01),
        'w_in': nrm(ks[11], (L, D_MODEL, IN_WIDTH), D_MODEL ** -0.5),
        'w_conv': nrm(ks[12], (L, SSD_CONV, SSD_CONV_CH), SSD_CONV ** -0.5),
        'b_conv': nrm(ks[13], (L, SSD_CONV_CH), 0.01),
        'dt_bias': dt0 + jnp.log(-jnp.expm1(-dt0)),
        'a_log': jnp.log(jax.random.uniform(ks[15], (L, SSD_HEADS), f32, 1.0, 16.0)),
        'd_skip': 1.0 + nrm(ks[16], (L, SSD_HEADS), 0.01),
        'g_ssd_out': 1.0 + nrm(ks[17], (L, SSD_D_INNER), 0.01),
        'b_forget': FORGET_BIAS_INIT + nrm(ks[18], (L, FOX_HEADS), 0.5),
        'g_mem': 1.0 + nrm(ks[19], (L, D_MODEL), 0.01),
        'w_mem_kv': nrm(ks[20], (L, D_MODEL, 2 * MEM_WIDTH), D_MODEL ** -0.5),
        'w_o_ssd': nrm(ks[21], (L, SSD_D_INNER, D_MODEL), SSD_D_INNER ** -0.5),
        'w_o_fox': nrm(ks[22], (L, FOX_WIDTH, D_MODEL), FOX_WIDTH ** -0.5),
        'w_o_mem': nrm(ks[23], (L, MEM_WIDTH, D_MODEL), MEM_WIDTH ** -0.5),
        'w_out': nrm(ks[24], (L, D_MODEL, D_MODEL), D_MODEL ** -0.5),
        'g_final': 1.0 + nrm(ks[25], (D_MODEL,), 0.01),
    }


def reference(x_prompt, x_sample, mem_prompt, cache_fox_k, cache_fox_v, cache_fox_logf, state_ssd,
              state_ssd_conv, cache_mem_k, cache_mem_v, g_norm, w_in, w_conv, b_conv, dt_bias, a_log,
              d_skip, g_ssd_out, b_forget, g_mem, w_mem_kv, w_o_ssd, w_o_fox, w_o_mem, w_out, g_final):
    f32 = jnp.float32
    xp, xs = x_prompt, x_sample
    bp = xp.shape[0]
    fk_p, fv_p, fl_p, ssd_p, conv_p, mk_p, mv_p = [], [], [], [], [], [], []
    fk_s, fv_s, fl_s, ssd_s, conv_s = [], [], [], [], []
    for l in range(DEPTH):
        lw = (g_norm[l], w_in[l], w_conv[l], b_conv[l], dt_bias[l], a_log[l], d_skip[l], g_ssd_out[l],
              b_forget[l], w_o_ssd[l], w_o_fox[l], w_o_mem[l], w_out[l])
        mk, mv = mem_kv(mem_prompt, g_mem[l], w_mem_kv[l])
        xp, c_new, h_new, k_new, v_new, lf_new = mixer_layer(
            xp, jnp.zeros((bp, SSD_CONV - 1, SSD_CONV_CH), xp.dtype),
            jnp.zeros((bp, SSD_HEADS, SSD_HEAD_DIM, SSD_D_STATE), f32),
            jnp.zeros((bp, 0, FOX_HEADS, FOX_HEAD_DIM), xp.dtype),
            jnp.zeros((bp, 0, FOX_HEADS, FOX_HEAD_DIM), xp.dtype),
            jnp.zeros((bp, 0, FOX_HEADS), f32), mk, mv, *lw, CHUNK, Q_BLOCK)
        fk_p.append(k_new); fv_p.append(v_new); fl_p.append(lf_new)
        ssd_p.append(h_new); conv_p.append(c_new); mk_p.append(mk); mv_p.append(mv)
        xs, c_new, h_new, k_new, v_new, lf_new = mixer_layer(
            xs, state_ssd_conv[l], state_ssd[l], cache_fox_k[l], cache_fox_v[l], cache_fox_logf[l],
            cache_mem_k[l], cache_mem_v[l], *lw, xs.shape[1], None)
        fk_s.append(k_new); fv_s.append(v_new); fl_s.append(lf_new)
        ssd_s.append(h_new); conv_s.append(c_new)
    y_prompt = rmsnorm(xp, g_final)
    y_sample = rmsnorm(xs, g_final)
    return (y_prompt, y_sample,
            jnp.stack(fk_p), jnp.stack(fv_p), jnp.stack(fl_p), jnp.stack(ssd_p), jnp.stack(conv_p),
            jnp.stack(mk_p), jnp.stack(mv_p),
            jnp.stack(fk_s), jnp.stack(fv_s), jnp.stack(fl_s), jnp.stack(ssd_s), jnp.stack(conv_s))
```

```python
import numpy as np
import concourse.bass as bass
import concourse.mybir as mybir
from concourse.bass_utils import run_bass_kernel_spmd
from contextlib import ExitStack

F32 = mybir.dt.float32
BF16 = mybir.dt.bfloat16
AF = mybir.ActivationFunctionType
ALU = mybir.AluOpType

import os as _os
NDS = int(_os.environ.get('KNDS', '22'))
NONESHOT = 73
D = 2048
EPS = 1e-6
INW = 23600
SPL = dict(z=0, xbc=2048, dt=5120, fq=5152, fk=7200, fv=9248, fg=11296, ff=13344, mq=13360, mg=15408,
           gs=17456, gf=19504, gm=21552)


class Res:
    __slots__ = ("name", "w", "r", "excl")

    def __init__(self, name, excl=False):
        self.name = name
        self.w = None
        self.r = []
        self.excl = excl


class Rot:
    def __init__(self, items):
        self.items = items
        self.i = 0

    def next(self):
        it = self.items[self.i]
        self.i = (self.i + 1) % len(self.items)
        return it


class Sched:
    def __init__(self, nc, stack, same_engine_sync=True):
        self.nc = nc
        self.engs = {"pe": nc.tensor, "act": nc.scalar, "dve": nc.vector, "pool": nc.gpsimd, "sp": nc.sync}
        self.sem = {k: stack.enter_context(nc.semaphore(k + "_s")) for k in ["pe", "act", "dve", "pool"]}
        self.cnt = {k: 0 for k in self.sem}
        self.seen = {k: {} for k in self.engs}
        self._dummy = [stack.enter_context(nc.semaphore("dummy%d" % i)) for i in range(int(_os.environ.get("KDUMMY", "0")))]
        self.dsems = [stack.enter_context(nc.semaphore("dq%d" % i)) for i in range(NDS + NONESHOT)]
        self.dval = [0] * (NDS + NONESHOT)
        self.dnext = 0
        self.oneshot_next = NDS
        self.same_engine_sync = same_engine_sync
        self.ninst = 0
        self.nwait = 0
        self.live_ranges = []
        self.last_range = None
        self.phase_res = []
        self.halted = False

    def pres(self, name):
        if self.last_range is None:
            lo, hi = self.prev_range
        else:
            lo, hi = self.last_range
        self.prev_range = (lo, hi)
        self.last_range = None
        r = Res(name)
        toks = {}
        for (l2, h2, tk) in self.live_ranges:
            if l2 < hi and lo < h2:
                for (k, v) in tk:
                    if toks.get(k, 0) < v:
                        toks[k] = v
        r.r = list(toks.items())
        self.phase_res.append((r, lo, hi))
        return r

    def phase_end(self):
        new = []
        for (r, lo, hi) in self.phase_res:
            toks = {}
            for (k, v) in ([r.w] if r.w is not None else []) + r.r:
                if toks.get(k, 0) < v:
                    toks[k] = v
            new.append((lo, hi, list(toks.items())))
        old = [e for e in self.live_ranges if not any(lo <= e[0] and e[1] <= hi for (lo, hi, _) in new)]
        self.live_ranges = old + new
        self.phase_res = []

    def _semobj(self, key):
        if isinstance(key, str):
            return self.sem[key]
        return self.dsems[key[1]]

    def _wait(self, eng, deps):
        seen = self.seen[eng]
        best = {}
        for (k, v) in deps:
            if best.get(k, 0) < v:
                best[k] = v
        for k, v in best.items():
            if k == eng:
                if eng == "pe" or not self.same_engine_sync:
                    continue
            if seen.get(k, 0) >= v:
                continue
            self.engs[eng].wait_ge(self._semobj(k), v)
            seen[k] = v
            self.nwait += 1

    def _deps(self, reads, writes, eng=None):
        deps = []
        for b in reads:
            if b.w is not None:
                deps.append(b.w)
            if b.excl:
                deps.extend(t for t in b.r if t[0] != eng)
        for b in writes:
            if b.w is not None:
                deps.append(b.w)
            deps.extend(b.r)
        return deps

    def _record(self, tok, reads, writes):
        for b in reads:
            if len(b.r) > 24:
                best = {}
                for (k, v) in b.r:
                    if best.get(k, 0) < v:
                        best[k] = v
                b.r = list(best.items())
            b.r.append(tok)
        for b in writes:
            b.w = tok
            b.r = []

    def op(self, eng, fn, reads=(), writes=()):
        if self.halted:
            return None
        self._wait(eng, self._deps(reads, writes, eng))
        inst = fn(self.engs[eng])
        self.cnt[eng] += 1
        inst.then_inc(self.sem[eng], 1)
        self._record((eng, self.cnt[eng]), reads, writes)
        self.ninst += 1
        return inst

    def mm(self, out, lhsT, rhs, start, stop, reads=(), writes=()):
        return self.op("pe", lambda e: e.matmul(out, lhsT, rhs, start=start, stop=stop), reads, writes)

    def dma(self, q, out, in_, reads=(), writes=(), **kw):
        if self.halted:
            return None
        deps = self._deps(reads, writes)
        if q == "pool":
            i = self.oneshot_next
            self.oneshot_next += 1
            assert i < NDS + NONESHOT, "out of one-shot DMA semaphores"
        else:
            i = self.dnext
            self.dnext = (self.dnext + 1) % NDS
        key = ("d", i)
        if self.dval[i] > 0:
            deps.append((key, self.dval[i]))
        self._wait(q, deps)
        inst = self.engs[q].dma_start(out=out, in_=in_, **kw)
        self.dval[i] += 16
        inst.then_inc(self.dsems[i], 16)
        tok = (key, self.dval[i])
        self._record(tok, reads, writes)
        self.ninst += 1
        return tok

    def finish(self):
        deps = []
        for i in range(NDS + NONESHOT):
            if self.dval[i] > 0:
                deps.append((("d", i), self.dval[i]))
        for k in self.cnt:
            if self.cnt[k] > 0:
                deps.append((k, self.cnt[k]))
        self._wait("sp", deps)


def weight_chunks():
    ch = {}
    ch["dtff"] = ("w_in", [(SPL["dt"], 32, 0), (SPL["ff"], 16, 32)], 48)
    for j in range(4):
        ch["z%d" % j] = ("w_in", [(SPL["z"] + 512 * j, 512, 0)], 512)
    for j in range(6):
        ch["xbc%d" % j] = ("w_in", [(SPL["xbc"] + 512 * j, 512, 0)], 512)
    for j in range(4):
        ch["wos%d" % j] = ("w_o_ssd", [(512 * j, 512, 0)], 512)
        ch["gs%d" % j] = ("w_in", [(SPL["gs"] + 512 * j, 512, 0)], 512)
    for j in range(4):
        for nm in ("fq", "fk", "fv", "fg"):
            ch["%s%d" % (nm, j)] = ("w_in", [(SPL[nm] + 512 * j, 512, 0)], 512)
    for j in range(4):
        ch["wof%d" % j] = ("w_o_fox", [(512 * j, 512, 0)], 512)
        ch["gf%d" % j] = ("w_in", [(SPL["gf"] + 512 * j, 512, 0)], 512)
    for j in range(4):
        for nm in ("mq", "mg"):
            ch["%s%d" % (nm, j)] = ("w_in", [(SPL[nm] + 512 * j, 512, 0)], 512)
    for j in range(4):
        ch["wom%d" % j] = ("w_o_mem", [(512 * j, 512, 0)], 512)
        ch["gm%d" % j] = ("w_in", [(SPL["gm"] + 512 * j, 512, 0)], 512)
    for j in range(4):
        ch["wout%d" % j] = ("w_out", [(512 * j, 512, 0)], 512)
    return ch


TILE_WORDER = (["dtff"] + ["z%d" % j for j in range(4)] + ["xbc%d" % j for j in range(6)]
               + [n for j in range(4) for n in ("wos%d" % j, "gs%d" % j)]
               + [n for j in range(4) for n in ("fq%d" % j, "fk%d" % j, "fv%d" % j, "fg%d" % j)]
               + [n for j in range(4) for n in ("wof%d" % j, "gf%d" % j)]
               + [n for j in range(4) for n in ("mq%d" % j, "mg%d" % j)]
               + [n for j in range(4) for n in ("wom%d" % j, "gm%d" % j)]
               + ["wout%d" % j for j in range(4)])
MKV_WORDER = ["mkv%d" % j for j in range(8)]


class StopBuild(Exception):
    pass


class Prog:
    def __init__(self, n_prompt_seq=2, prompt_tiles=4, do_sample=True):
        self.n_prompt_seq = n_prompt_seq
        self.prompt_tiles = prompt_tiles
        self.do_sample = do_sample
        self.nc = bass.Bass("TRN2", target_bir_lowering=False)

    def din(self, name, shape, dt=F32):
        return self.nc.dram_tensor(name, list(shape), dt, kind="ExternalInput").ap()

    def dout(self, name, shape, dt=F32):
        return self.nc.dram_tensor(name, list(shape), dt, kind="ExternalOutput").ap()

    def dscr(self, name, shape, dt=BF16):
        return self.nc.dram_tensor(name, list(shape), dt, kind="Internal").ap()

    def sb(self, st, name, shape, dt):
        self._uid = getattr(self, "_uid", 0) + 1
        t = st.enter_context(self.nc.sbuf_tensor("sb%d_%s" % (self._uid, name), list(shape), dt))
        nbytes = int(np.prod(shape[1:])) * (2 if dt == BF16 else 4)
        cur = getattr(self, "_sb_cur", 0)
        off = (cur + 31) // 32 * 32
        self._sb_cur = off + nbytes
        st.callback(self._sb_release, cur)
        if hasattr(self, "S"):
            self.S.last_range = (off - 32, off + nbytes + 32)
        return t

    def _sb_release(self, cur):
        self._sb_cur = cur

    def build(self):
        nc = self.nc
        NP = 2
        I = {}
        I["xp"] = self.din("xp", [NP, 2048, D])
        I["xs"] = self.din("xs", [64, D])
        I["memp"] = self.din("memp", [NP, 256, D])
        I["ck"] = self.din("ck", [1024, D])
        I["cv"] = self.din("cv", [1024, D])
        I["clf"] = self.din("clf", [1024, 16])
        I["sst"] = self.din("sst", [2048, 128])
        I["sconv"] = self.din("sconv", [3, 3072])
        I["cmk"] = self.din("cmk", [256, D])
        I["cmv"] = self.din("cmv", [256, D])
        I["w_in"] = self.din("w_in", [D, INW])
        I["w_mem_kv"] = self.din("w_mem_kv", [D, 4096])
        for n in ("w_o_ssd", "w_o_fox", "w_o_mem", "w_out"):
            I[n] = self.din(n, [D, D])
        I["g_final"] = self.din("g_final", [D])
        I["gcols"] = self.din("gcols", [128, 48])
        I["wconv_c"] = self.din("wconv_c", [128, 24 * 4])
        I["bconv_c"] = self.din("bconv_c", [128, 24])
        I["hb"] = self.din("hb", [128, 112])
        I["cst"] = self.din("cst", [128, 4 * 128])
        I["selc"] = self.din("selc", [16, 16 * 128])
        O = {}
        O["yp"] = self.dout("yp", [NP, 2048, D])
        O["ys"] = self.dout("ys", [64, D])
        O["fkp"] = self.dout("fkp", [NP, 2048, D])
        O["fvp"] = self.dout("fvp", [NP, 2048, D])
        O["flp"] = self.dout("flp", [NP, 2048, 16])
        O["ssdp"] = self.dout("ssdp", [NP, 2048, 128])
        O["convp"] = self.dout("convp", [NP, 3, 3072])
        O["mkp"] = self.dout("mkp", [NP, 256, D])
        O["mvp"] = self.dout("mvp", [NP, 256, D])
        O["fks"] = self.dout("fks", [64, D])
        O["fvs"] = self.dout("fvs", [64, D])
        O["fls"] = self.dout("fls", [64, 16])
        O["ssds"] = self.dout("ssds", [2048, 128])
        O["convs"] = self.dout("convs", [3, 3072])
        self.I, self.O = I, O

        chunks = weight_chunks()
        for j in range(8):
            chunks["mkv%d" % j] = ("w_mem_kv", [(512 * j, 512, 0)], 512)
        self.chunks = chunks
        self.wscr = {n: self.dscr("wb_" + n, [128, 16, c[2]]) for n, c in chunks.items()}
        self.r_wscr = {n: Res("wb_" + n) for n in chunks}
        self.kT_hist = self.dscr("kT_hist", [16, 128, 2048])
        self.v_hist = self.dscr("v_hist", [16, 128, 16, 128])
        self.memKT_d = self.dscr("memKT_d", [128, 16, 256])
        self.memV_d = self.dscr("memV_d", [128, 2, 2048])
        self.r_khist = Res("khist")
        self.r_vhist = Res("vhist")
        self.r_memKT_d = Res("memKT_d")
        self.r_memV_d = Res("memV_d")

        with ExitStack() as st:
            self.S = S = Sched(nc, st, same_engine_sync=(_os.environ.get('KSES', '1') == '1'))
            sb = lambda name, shape, dt: self.sb(st, name, shape, dt)
            self.cst_f = sb("cst_f", [128, 512], F32)
            self.cst_b = sb("cst_b", [128, 512], BF16)
            self.sel_b = sb("sel_b", [16, 2048], BF16)
            self.gcols = sb("gcols", [128, 48], F32)
            self.wconv_c = sb("wconv_c", [128, 96], F32)
            self.bconv_c = sb("bconv_c", [128, 24], F32)
            self.hb = sb("hb", [128, 112], F32)
            self.A_b = sb("A_b", [128, 32], F32)
            self.cc = sb("cc", [128, 4], F32)
            self.r_cst = Res("cst")
            self.ident_f = self.cst_f[:, 0:128]
            self.triU_f = self.cst_f[:, 128:256]
            self.triLs_f = self.cst_f[:, 256:384]
            self.ones_f = self.cst_f[:, 384:512]
            self.ident_b = self.cst_b[:, 0:128]
            self.triU_b = self.cst_b[:, 128:256]
            self.ones_b = self.cst_b[:, 384:512]
            self.stateT = sb("stateT", [128, 2048], F32)
            self.hinT = sb("hinT", [128, 2048], BF16)
            self.convhist = sb("convhist", [128, 24, 3], F32)
            self.negc = sb("negc", [128, 17, 16], F32)
            self.carry = sb("carry", [128, 16], F32)
            self.r_state = Res("state")
            self.r_hinT = Res("hinT")
            self.r_convhist = Res("convhist")
            self.r_negc = Res("negc")
            self.r_carry = Res("carry")
            self.hT = sb("hT", [128, 16, 512], BF16)
            self.mergedT = sb("mergedT", [128, 16, 512], BF16)
            self.lf_tok = sb("lf_tok", [128, 4, 16], F32)
            self.cT_bf = sb("cT_bf", [16, 512], BF16)
            self.r_hT = Res("hT")
            self.r_merged = Res("merged")
            self.r_lf = Res("lf")
            self.r_cT = Res("cT")
            self.wbufs = Rot([(sb("wbuf%d" % i, [128, 16, 512], BF16), Res("wbuf%d" % i)) for i in range(2)])
            self.ps = st.enter_context(nc.psum_tensor("ps", [128, 4096], F32))
            self.banks = [(self.ps[:, i * 512:(i + 1) * 512], Res("bank%d" % i, excl=True)) for i in range(8)]
            self.rot = Rot(self.banks[0:4])

            S.dma("sp", self.cst_f[:], I["cst"], writes=[self.r_cst])
            with ExitStack() as ist:
                sel_f = self.sb(ist, "sel_f", [16, 2048], F32)
                r_self = S.pres("sel_f")
                S.dma("sp", sel_f[:], I["selc"], writes=[r_self])
                S.op("dve", lambda e: e.tensor_copy(self.sel_b[:], sel_f[:]), reads=[r_self], writes=[self.r_cst])
            S.phase_end()
            S.dma("sp", self.gcols[:], I["gcols"], writes=[self.r_cst])
            S.dma("sp", self.wconv_c[:], I["wconv_c"], writes=[self.r_cst])
            S.dma("sp", self.bconv_c[:], I["bconv_c"], writes=[self.r_cst])
            S.dma("sp", self.hb[:], I["hb"], writes=[self.r_cst])
            S.op("dve", lambda e: e.tensor_copy(self.cst_b[:], self.cst_f[:]), reads=[self.r_cst], writes=[self.r_cst])
            S.op("dve", lambda e: e.memset(self.cc[:, 0:1], EPS), writes=[self.r_cst])
            S.op("dve", lambda e: e.memset(self.cc[:, 1:2], 1.0), writes=[self.r_cst])
            S.op("act", lambda e: e.activation(self.A_b[:], self.hb[:, 32:64], AF.Exp), reads=[self.r_cst], writes=[self.r_cst])
            S.op("dve", lambda e: e.tensor_scalar(self.A_b[:], self.A_b[:], -1.0, None, ALU.mult), reads=[self.r_cst], writes=[self.r_cst])

            plan = []
            for s in range(self.n_prompt_seq):
                plan += MKV_WORDER
                for t in range(self.prompt_tiles):
                    plan += TILE_WORDER
            if self.do_sample:
                plan += TILE_WORDER
            self.wplan = plan
            self.wpos = 0
            self.wloaded = {}
            self.conv_done = set()
            self.conv_next = 0
            self._convert_upto(0)

            try:
                self.body()
            except StopBuild:
                pass
            S.finish()
        return nc

    def ckpt(self, name):
        import os
        if os.environ.get("KSTOP") == name:
            raise StopBuild()

    def body(self):
        if True:
            I, O, S = self.I, self.O, self.S
            self.ckpt("stage0")
            tiles = []
            for s in range(self.n_prompt_seq):
                for t in range(self.prompt_tiles):
                    tiles.append(dict(kind="p", s=s, t=t, TT=512, BS=128, blk0=4 * t, x_src=I["xp"][s, 512 * t:512 * (t + 1), :],
                                      y_out=O["yp"][s, 512 * t:512 * (t + 1), :],
                                      fk_out=O["fkp"][s, 512 * t:512 * (t + 1), :],
                                      fv_out=O["fvp"][s, 512 * t:512 * (t + 1), :],
                                      fl_out=O["flp"][s, 512 * t:512 * (t + 1), :]))
            if self.do_sample:
                tiles.append(dict(kind="s", s=0, t=0, TT=64, BS=64, blk0=8, x_src=I["xs"], y_out=O["ys"], fk_out=O["fks"],
                                  fv_out=O["fvs"], fl_out=O["fls"]))
            for i, tl in enumerate(tiles):
                if tl["kind"] == "p" and tl["t"] == 0:
                    self.seq_init_prompt()
                    self.mem_prologue_prompt(tl["s"])
                    self.ckpt("memp")
                if tl["kind"] == "s":
                    self.sample_prologue()
                    if S.halted:
                        S.halted = False
                        raise StopBuild()
                    self.ckpt("sprol")
                if i == 0:
                    self.tile_t1(tl["TT"], tl["BS"], tl["x_src"])
                nxt = tiles[i + 1] if i + 1 < len(tiles) else None
                self.tile(tl["TT"], tl["BS"], tl["blk0"], tl["x_src"], tl["y_out"], tl["fk_out"], tl["fv_out"], tl["fl_out"], nxt)
                if tl["kind"] == "p" and tl["t"] == self.prompt_tiles - 1:
                    self.seq_epilogue(O["ssdp"][tl["s"]], O["convp"][tl["s"]])
                if tl["kind"] == "s":
                    self.seq_epilogue(O["ssds"], O["convs"])

    def wget(self, name):
        assert self.wplan[self.wpos] == name, (self.wplan[self.wpos], name)
        if self.wpos not in self.wloaded:
            self._wload(self.wpos)
        cur = self.wloaded.pop(self.wpos)
        self.wpos += 1
        if self.wpos < len(self.wplan) and self.wpos not in self.wloaded:
            self._wload(self.wpos)
        return cur

    def _convert_upto(self, pos):
        LOOK = int(_os.environ.get("KLOOK", "8"))
        while self.conv_next < len(self.wplan) and self.conv_next <= pos + LOOK:
            n = self.wplan[self.conv_next]
            self.conv_next += 1
            if n in self.conv_done:
                continue
            self.conv_done.add(n)
            src, parts, tot = self.chunks[n]
            for (c0, ncol, d0) in parts:
                self.S.dma("pool", self.wscr[n][:, :, d0:d0 + ncol],
                           self.I[src][:, c0:c0 + ncol].rearrange("(fc p) n -> p fc n", p=128),
                           writes=[self.r_wscr[n]])

    def _wload(self, pos):
        self._convert_upto(pos)
        n = self.wplan[pos]
        buf, r = self.wbufs.next()
        ncol = self.chunks[n][2]
        self.S.dma("sp", buf[:, :, 0:ncol], self.wscr[n], reads=[self.r_wscr[n]], writes=[r])
        self.wloaded[pos] = (buf, r)

    def seq_init_prompt(self):
        S = self.S
        S.op("pool", lambda e: e.memset(self.stateT[:], 0.0), writes=[self.r_state])
        S.op("pool", lambda e: e.memset(self.hinT[:], 0.0), writes=[self.r_hinT])
        S.op("pool", lambda e: e.memset(self.convhist[:], 0.0), writes=[self.r_convhist])
        S.op("pool", lambda e: e.memset(self.carry[:], 0.0), writes=[self.r_carry])

    def norm_transpose(self, ph, src, BS, dstT, c0, gcol, r_dst, q="sp"):
        xs, r_xs = self.norm_load(ph, src, BS, q)
        self.norm_tr(xs, r_xs, BS, dstT, c0, gcol, r_dst)

    def norm_load(self, ph, src, BS, q="sp"):
        S = self.S
        xs, r_xs = ph["xs"].next()
        S.dma(q, xs[:BS, :], src, writes=[r_xs])
        junk, r_junk = ph["junk"]
        sm, r_sm = ph["small"].next()
        S.op("act", lambda e: e.activation(junk[:BS, :], xs[:BS, :], AF.Square, accum_out=sm[:BS, 0:1]),
             reads=[r_xs], writes=[r_junk, r_sm])
        S.op("act", lambda e: e.activation(sm[:BS, 1:2], sm[:BS, 0:1], AF.Sqrt, bias=self.cc[:BS, 0:1], scale=1.0 / D),
             reads=[r_sm, self.r_cst], writes=[r_sm])
        S.op("dve", lambda e: e.reciprocal(sm[:BS, 2:3], sm[:BS, 1:2]), reads=[r_sm], writes=[r_sm])
        S.op("dve", lambda e: e.tensor_scalar(xs[:BS, :], xs[:BS, :], sm[:BS, 2:3], None, ALU.mult),
             reads=[r_xs, r_sm], writes=[r_xs])
        return xs, r_xs

    def norm_tr(self, xs, r_xs, BS, dstT, c0, gcol, r_dst):
        S = self.S
        for q in range(4):
            bank, r_b = self.rot.next()
            for k in range(4):
                fc = q * 4 + k
                S.op("pe", lambda e: e.transpose(bank[:, k * BS:(k + 1) * BS], xs[:BS, fc * 128:(fc + 1) * 128],
                                                 self.ident_f[:BS, :BS]),
                     reads=[r_xs, self.r_cst], writes=[r_b])
            S.op("dve", lambda e: e.tensor_tensor(
                dstT[:, q * 4:(q + 1) * 4, c0:c0 + BS],
                bank[:, 0:4 * BS].rearrange("p (k b) -> p k b", k=4),
                gcol[:, q * 4:(q + 1) * 4].unsqueeze(2).to_broadcast([128, 4, BS]), ALU.mult),
                reads=[r_b, self.r_cst], writes=[r_dst])

    def proj_tok(self, lhsT_tile, c0, BS, w, r_w, ncol, reads):
        bank, r_b = self.rot.next()
        for fc in range(16):
            self.S.mm(bank[:BS, 0:ncol], lhsT_tile[:, fc, c0:c0 + BS], w[:, fc, 0:ncol], start=(fc == 0), stop=(fc == 15),
                      reads=reads + [r_w], writes=[r_b])
        return bank, r_b

    def proj_feat(self, w, r_w, wc0, rhs_tile, TT, reads, bank=None):
        if bank is None:
            bank, r_b = self.rot.next()
        else:
            bank, r_b = bank
        for fc in range(16):
            self.S.mm(bank[:, 0:TT], w[:, fc, wc0:wc0 + 128], rhs_tile[:, fc, 0:TT], start=(fc == 0), stop=(fc == 15),
                      reads=reads + [r_w], writes=[r_b])
        return bank, r_b

    def mem_prologue_prompt(self, s):
        S, nc = self.S, self.nc
        with ExitStack() as pst:
            sb = lambda name, shape, dt: self.sb(pst, name, shape, dt)
            ph = {
                "xs": Rot([(sb("mp_xs%d" % i, [128, 2048], F32), S.pres("mp_xs")) for i in range(2)]),
                "junk": (sb("mp_junk", [128, 2048], BF16), S.pres("mp_junk")),
                "small": Rot([(sb("mp_sm%d" % i, [128, 4], F32), S.pres("mp_sm")) for i in range(2)]),
            }
            mT = sb("mp_mT", [128, 16, 256], BF16)
            r_mT = S.pres("mp_mT")
            memKT = sb("mp_memKT", [128, 16, 256], BF16)
            r_memKT = S.pres("mp_memKT")
            memV = sb("mp_memV", [128, 2, 2048], BF16)
            r_memV = S.pres("mp_memV")
            stg = Rot([(sb("mp_stg%d" % i, [128, 512], F32), S.pres("mp_stg")) for i in range(3)])
            for mb in range(2):
                self.norm_transpose(ph, self.I["memp"][s, mb * 128:(mb + 1) * 128, :], 128, mT, mb * 128,
                                    self.gcols[:, 32:48], r_mT)
            for j in range(8):
                w, r_w = self.wget("mkv%d" % j)
                if j < 4:
                    for cc in range(4):
                        bank, r_b = self.proj_feat(w, r_w, cc * 128, mT, 256, [r_mT])
                        S.op("act", lambda e: e.copy(memKT[:, j * 4 + cc, :], bank[:, 0:256]), reads=[r_b], writes=[r_memKT])
                for mb in range(2):
                    bank, r_b = self.proj_tok(mT, mb * 128, 128, w, r_w, 512, [r_mT])
                    sg, r_sg = stg.next()
                    S.op("dve", lambda e: e.tensor_copy(sg[:], bank[:, :]), reads=[r_b], writes=[r_sg])
                    if j < 4:
                        S.dma("act", self.O["mkp"][s, mb * 128:(mb + 1) * 128, j * 512:(j + 1) * 512], sg[:], reads=[r_sg])
                    else:
                        jj = j - 4
                        S.dma("act", self.O["mvp"][s, mb * 128:(mb + 1) * 128, jj * 512:(jj + 1) * 512], sg[:], reads=[r_sg])
                        S.op("act", lambda e: e.copy(memV[:, mb, jj * 512:(jj + 1) * 512], bank[:, :]), reads=[r_b],
                             writes=[r_memV])
            S.dma("act", self.memKT_d, memKT[:], reads=[r_memKT], writes=[self.r_memKT_d])
            S.dma("act", self.memV_d, memV[:], reads=[r_memV], writes=[self.r_memV_d])
        S.phase_end()

    def sample_prologue(self):
        S, I = self.S, self.I
        with ExitStack() as pst:
            sb = lambda name, shape, dt: self.sb(pst, name, shape, dt)
            ld = Rot([(sb("sp_ld%d" % i, [128, 2048], F32), S.pres("sp_ld")) for i in range(2)])
            stg = Rot([(sb("sp_stg%d" % i, [128, 16, 128], BF16), S.pres("sp_stg")) for i in range(2)])
            memKT = sb("sp_memKT", [128, 16, 256], BF16)
            r_memKT = S.pres("sp_memKT")
            lfh = sb("sp_lfh", [128, 8, 16], F32)
            r_lfh = S.pres("sp_lfh")
            ctmp = sb("sp_ctmp", [128, 8, 16], F32)
            r_ctmp = S.pres("sp_ctmp")
            for q in range(4):
                x, r_x = ld.next()
                S.dma("sp", x[:, 0:512].rearrange("p (a n) -> p a n", a=4),
                      I["sst"][q * 512:(q + 1) * 512, :].rearrange("(a p) n -> p a n", p=128), writes=[r_x])
                bank, r_b = self.rot.next()
                for a in range(4):
                    S.op("pe", lambda e: e.transpose(bank[:, a * 128:(a + 1) * 128], x[:, a * 128:(a + 1) * 128], self.ident_f),
                         reads=[r_x, self.r_cst], writes=[r_b])
                S.op("dve", lambda e: e.tensor_copy(self.stateT[:, q * 512:(q + 1) * 512], bank[:, :]), reads=[r_b],
                     writes=[self.r_state])
                S.op("act", lambda e: e.copy(self.hinT[:, q * 512:(q + 1) * 512], bank[:, :]), reads=[r_b], writes=[self.r_hinT])
            if _os.environ.get("KSTOP") == "sp_a": S.halted = True
            for r in range(3):
                S.dma("sp", self.convhist[:, :, r], I["sconv"][r].rearrange("(c p) -> p c", p=128), writes=[self.r_convhist],
                      allow_slow_non_contiguous=True)
            if _os.environ.get("KSTOP") == "sp_b": S.halted = True
            for mb in range(2):
                x, r_x = ld.next()
                S.dma("sp", x[:], I["cmk"][mb * 128:(mb + 1) * 128, :], writes=[r_x])
                for q in range(4):
                    bank, r_b = self.rot.next()
                    for k in range(4):
                        c = q * 4 + k
                        S.op("pe", lambda e: e.transpose(bank[:, k * 128:(k + 1) * 128], x[:, c * 128:(c + 1) * 128], self.ident_f),
                             reads=[r_x, self.r_cst], writes=[r_b])
                    S.op("act", lambda e: e.copy(memKT[:, q * 4:(q + 1) * 4, mb * 128:(mb + 1) * 128],
                                                 bank[:, :].rearrange("p (k b) -> p k b", k=4)), reads=[r_b], writes=[r_memKT])
            S.dma("act", self.memKT_d, memKT[:], reads=[r_memKT], writes=[self.r_memKT_d])
            vstg = Rot([(sb("sp_vstg%d" % i, [128, 2048], BF16), S.pres("sp_vstg")) for i in range(2)])
            for mb in range(2):
                x, r_x = ld.next()
                S.dma("sp", x[:], I["cmv"][mb * 128:(mb + 1) * 128, :], writes=[r_x])
                vs, r_vs = vstg.next()
                S.op("pool", lambda e: e.tensor_copy(vs[:], x[:]), reads=[r_x], writes=[r_vs])
                S.dma("act", self.memV_d[:, mb, :], vs[:], reads=[r_vs], writes=[self.r_memV_d])
            if _os.environ.get("KSTOP") == "sp_c": S.halted = True
            for blk in range(8):
                x, r_x = ld.next()
                S.dma("sp", x[:], I["ck"][blk * 128:(blk + 1) * 128, :], writes=[r_x])
                sg, r_sg = stg.next()
                for q in range(4):
                    bank, r_b = self.rot.next()
                    for k in range(4):
                        h = q * 4 + k
                        S.op("pe", lambda e: e.transpose(bank[:, k * 128:(k + 1) * 128], x[:, h * 128:(h + 1) * 128], self.ident_f),
                             reads=[r_x, self.r_cst], writes=[r_b])
                    S.op("act" if q % 2 == 0 else "dve",
                         (lambda e: e.copy(sg[:, q * 4:(q + 1) * 4, :], bank[:, :].rearrange("p (k b) -> p k b", k=4))) if q % 2 == 0 else
                         (lambda e: e.tensor_copy(sg[:, q * 4:(q + 1) * 4, :], bank[:, :].rearrange("p (k b) -> p k b", k=4))),
                         reads=[r_b], writes=[r_sg])
                S.dma("act", self.kT_hist[:, :, blk * 128:(blk + 1) * 128].rearrange("h d t -> d h t"), sg[:],
                      reads=[r_sg], writes=[self.r_khist])
                x, r_x = ld.next()
                S.dma("sp", x[:], I["cv"][blk * 128:(blk + 1) * 128, :], writes=[r_x])
                vs, r_vs = vstg.next()
                S.op("pool", lambda e: e.tensor_copy(vs[:], x[:]), reads=[r_x], writes=[r_vs])
                S.dma("act", self.v_hist[:, :, blk, :].rearrange("h p d -> p h d"), vs[:].rearrange("p (h d) -> p h d", h=16),
                      reads=[r_vs], writes=[self.r_vhist])
            if _os.environ.get("KSTOP") == "sp_d": S.halted = True
            S.dma("sp", lfh[:], I["clf"].rearrange("(b p) h -> p b h", p=128), writes=[r_lfh])
            S.op("pool", lambda e: e.memset(self.carry[:], 0.0), writes=[self.r_carry])
            self.cumsum_blocks(lfh, r_lfh, 128, 8, 0, ctmp, r_ctmp)
        S.phase_end()

    def cumsum_blocks(self, lf, r_lf, BS, NB, blk0, ctmp, r_ctmp):
        S = self.S
        bank, r_b = self.rot.next()
        n = NB * 16
        S.mm(bank[:BS, 0:n], self.triU_f[:BS, :BS], lf[:BS, 0:NB, :].rearrange("p b h -> p (b h)"), start=True, stop=True,
             reads=[r_lf, self.r_cst], writes=[r_b])
        S.mm(bank[:, 256:256 + n], self.ones_f[:BS, :], lf[:BS, 0:NB, :].rearrange("p b h -> p (b h)"), start=True, stop=True,
             reads=[r_lf, self.r_cst], writes=[r_b])
        S.op("dve", lambda e: e.tensor_copy(ctmp[:, 0:NB, :].rearrange("p b h -> p (b h)"), bank[:, 256:256 + n]),
             reads=[r_b], writes=[r_ctmp])
        for b in range(NB):
            S.op("dve", lambda e: e.scalar_tensor_tensor(self.negc[:BS, blk0 + b, :], bank[:BS, b * 16:(b + 1) * 16], -1.0,
                                                         self.carry[:BS, :], ALU.mult, ALU.subtract),
                 reads=[r_b, self.r_carry], writes=[self.r_negc])
            S.op("dve", lambda e: e.tensor_tensor(self.carry[:], self.carry[:], ctmp[:, b, :], ALU.add),
                 reads=[self.r_carry, r_ctmp], writes=[self.r_carry])

    def seq_epilogue(self, ssd_out, conv_out):
        S = self.S
        with ExitStack() as pst:
            sb = lambda name, shape, dt: self.sb(pst, name, shape, dt)
            stg = Rot([(sb("ep_stg%d" % i, [128, 4, 128], F32), S.pres("ep_stg")) for i in range(2)])
            for q in range(4):
                bank, r_b = self.rot.next()
                for a in range(4):
                    c = q * 4 + a
                    S.op("pe", lambda e: e.transpose(bank[:, a * 128:(a + 1) * 128], self.stateT[:, c * 128:(c + 1) * 128], self.ident_f),
                         reads=[self.r_state, self.r_cst], writes=[r_b])
                sg, r_sg = stg.next()
                S.op("dve", lambda e: e.tensor_copy(sg[:], bank[:, :].rearrange("p (a n) -> p a n", a=4)), reads=[r_b], writes=[r_sg])
                S.dma("act", ssd_out[q * 512:(q + 1) * 512, :].rearrange("(a p) n -> p a n", p=128), sg[:], reads=[r_sg])
            for r in range(3):
                S.dma("act", conv_out[r].rearrange("(c p) -> p c", p=128), self.convhist[:, :, r], reads=[self.r_convhist],
                      allow_slow_non_contiguous=True)
        S.phase_end()

    def branch_project(self, ph, yT, r_yT, wname, gname, first, TT):
        S = self.S
        for j in range(4):
            w, r_w = self.wget("%s%d" % (wname, j))
            pbanks = []
            for cc in range(4):
                pbanks.append(self.proj_feat(w, r_w, cc * 128, yT, TT, [r_yT], bank=self.banks[4 + cc]))
            g, r_g = self.wget("%s%d" % (gname, j))
            for cc in range(4):
                oc = j * 4 + cc
                gb, r_gb = self.proj_feat(g, r_g, cc * 128, self.hT, TT, [self.r_hT])
                sg, r_sg = ph["sig"].next()
                S.op("act", lambda e: e.activation(sg[:, 0:TT], gb[:, 0:TT], AF.Sigmoid), reads=[r_gb], writes=[r_sg])
                pb, r_pb = pbanks[cc]
                if first:
                    S.op("dve", lambda e: e.tensor_tensor(self.mergedT[:, oc, 0:TT], pb[:, 0:TT], sg[:, 0:TT], ALU.mult),
                         reads=[r_pb, r_sg], writes=[self.r_merged])
                else:
                    S.op("dve", lambda e: e.tensor_tensor(sg[:, 0:TT], pb[:, 0:TT], sg[:, 0:TT], ALU.mult),
                         reads=[r_pb, r_sg], writes=[r_sg])
                    S.op("dve", lambda e: e.tensor_tensor(self.mergedT[:, oc, 0:TT], self.mergedT[:, oc, 0:TT], sg[:, 0:TT], ALU.add),
                         reads=[r_sg, self.r_merged], writes=[self.r_merged])

    def tile_t1(self, TT, BS, x_src):
        S = self.S
        NB = TT // BS
        with ExitStack() as pst:
            sb = lambda name, shape, dt: self.sb(pst, name, shape, dt)
            ph = {
                "xs": Rot([(sb("t1_xs%d" % i, [128, 2048], F32), S.pres("t1_xs")) for i in range(2)]),
                "junk": (sb("t1_junk", [128, 2048], BF16), S.pres("t1_junk")),
                "small": Rot([(sb("t1_sm%d" % i, [128, 4], F32), S.pres("t1_sm")) for i in range(2)]),
            }
            for tb in range(NB):
                self.norm_transpose(ph, x_src[tb * BS:(tb + 1) * BS, :], BS, self.hT, tb * BS, self.gcols[:, 0:16], self.r_hT, q="act")
        S.phase_end()

    def tile(self, TT, BS, blk0, x_src, y_out, fk_out, fv_out, fl_out, nxt):
        NB = TT // BS
        self.ckpt("t1")
        self.tile_ssd(TT, BS, NB, blk0, fl_out)
        self.ckpt("ssd")
        self.tile_fox(TT, BS, NB, blk0, fk_out, fv_out)
        self.ckpt("fox")
        self.tile_mem(TT, BS, NB, nxt)
        self.ckpt("mem")
        self.tile_final(TT, BS, NB, x_src, y_out)
        self.ckpt("final")

    def tile_ssd(self, TT, BS, NB, blk0, fl_out):
        S = self.S
        hT = self.hT
        with ExitStack() as pst:
            sb = lambda name, shape, dt: self.sb(pst, name, shape, dt)
            ph = {"sig": Rot([(sb("s_sig%d" % i, [128, 512], BF16), S.pres("s_sig")) for i in range(2)])}
            yT = sb("s_yT", [128, 16, 512], BF16); r_yT = S.pres("s_yT")
            zs = sb("s_zs", [128, 4, 2048], BF16); r_zs = S.pres("s_zs")
            x_tok = sb("s_xtok", [128, 4, 2048], BF16); r_xtok = S.pres("s_xtok")
            BT = sb("s_BT", [128, 4, 512], BF16); r_BT = S.pres("s_BT")
            CT = sb("s_CT", [128, 4, 512], BF16); r_CT = S.pres("s_CT")
            B_tok = sb("s_Btok", [128, 4, 512], BF16); r_Btok = S.pres("s_Btok")
            sm = sb("s_sm", [128, 12, 128], F32); r_sm = S.pres("s_sm")
            dtr, e1, dt, dtA, acum, ea, tot, dtot, wend, ffb, e2, lfn = [sm[:, i, :] for i in range(12)]
            ctmp = sb("s_ctmp", [128, 4, 16], F32); r_ctmp = S.pres("s_ctmp")
            xp = Rot([(sb("s_xp%d" % i, [128, 520], F32), S.pres("s_xp")) for i in range(2)])
            cacc = Rot([(sb("s_cacc%d" % i, [128, 512], F32), S.pres("s_cacc")) for i in range(2)])
            xc = Rot([(sb("s_xc%d" % i, [128, 512], BF16), S.pres("s_xc")) for i in range(3)])
            xdt = sb("s_xdt", [128, 2048], BF16); r_xdt = S.pres("s_xdt")
            xw = sb("s_xw", [128, 2048], BF16); r_xw = S.pres("s_xw")
            yblk = [(sb("s_yblk%d" % i, [128, 2048], BF16), S.pres("s_yblk")) for i in range(2)]
            yn = sb("s_yn", [128, 2048], BF16); r_yn = S.pres("s_yn")
            rhsD1 = sb("s_rhsD", [128, 8, 128], F32)
            r_rhsD1 = S.pres("s_rhsD")
            scanb = Rot([(rhsD1, r_rhsD1, sb("s_E%d" % i, [128, 8, 128], BF16), S.pres("s_E"),
                          sb("s_MT%d" % i, [128, 8, 128], BF16), S.pres("s_MT")) for i in range(2)])
            CBm = sb("s_CBm", [128, 4, 128], BF16); r_CBm = S.pres("s_CBm")
            tmpg = Rot([(sb("s_tmpg%d" % i, [128, 512], F32), S.pres("s_tmpg")) for i in range(4)])
            st4 = sb("s_st4", [128, 4], F32); r_st4 = S.pres("s_st4")

            n32 = NB * 32
            w, r_w = self.wget("dtff")
            for tb in range(NB):
                bank, r_b = self.proj_tok(hT, tb * BS, BS, w, r_w, 48, [self.r_hT])
                S.op("dve", lambda e: e.tensor_tensor(dtr[:BS, tb * 32:(tb + 1) * 32], bank[:BS, 0:32], self.hb[:BS, 0:32], ALU.add),
                     reads=[r_b, self.r_cst], writes=[r_sm])
                S.op("dve", lambda e: e.tensor_tensor(ffb[:BS, tb * 16:(tb + 1) * 16], bank[:BS, 32:48], self.hb[:BS, 96:112], ALU.add),
                     reads=[r_b, self.r_cst], writes=[r_sm])
            n16 = NB * 16
            S.op("act", lambda e: e.activation(e1[:BS, 0:n32], dtr[:BS, 0:n32], AF.Exp), reads=[r_sm], writes=[r_sm])
            S.op("act", lambda e: e.activation(e2[:BS, 0:n16], ffb[:BS, 0:n16], AF.Exp, scale=-1.0), reads=[r_sm], writes=[r_sm])
            S.op("act", lambda e: e.activation(dt[:BS, 0:n32], e1[:BS, 0:n32], AF.Ln, bias=self.cc[:BS, 1:2]), reads=[r_sm, self.r_cst], writes=[r_sm])
            S.op("act", lambda e: e.activation(lfn[:BS, 0:n16], e2[:BS, 0:n16], AF.Ln, bias=self.cc[:BS, 1:2]), reads=[r_sm, self.r_cst], writes=[r_sm])
            S.op("dve", lambda e: e.tensor_scalar(self.lf_tok[:BS, 0:NB, :].rearrange("p b h -> p (b h)"), lfn[:BS, 0:n16], -1.0, None, ALU.mult),
                 reads=[r_sm], writes=[self.r_lf])
            S.dma("act", fl_out.rearrange("(b p) h -> p b h", p=BS), self.lf_tok[:BS, 0:NB, :], reads=[self.r_lf])
            for tb in range(NB):
                S.op("dve", lambda e: e.tensor_tensor(dtA[:BS, tb * 32:(tb + 1) * 32], dt[:BS, tb * 32:(tb + 1) * 32], self.A_b[:BS, :], ALU.mult),
                     reads=[r_sm, self.r_cst], writes=[r_sm])
            for j in range(4):
                w, r_w = self.wget("z%d" % j)
                for tb in range(NB):
                    bank, r_b = self.proj_tok(hT, tb * BS, BS, w, r_w, 512, [self.r_hT])
                    S.op("act", lambda e: e.activation(zs[:BS, tb, j * 512:(j + 1) * 512], bank[:BS, :], AF.Silu), reads=[r_b], writes=[r_zs])
            self.cumsum_blocks(self.lf_tok, self.r_lf, BS, NB, blk0, ctmp, r_ctmp)
            bank, r_b = self.rot.next()
            S.mm(bank[:BS, 0:n32], self.triU_f[:BS, :BS], dtA[:BS, 0:n32], start=True, stop=True, reads=[r_sm, self.r_cst], writes=[r_b])
            S.mm(bank[:, 256:256 + n32], self.ones_f[:BS, :], dtA[:BS, 0:n32], start=True, stop=True, reads=[r_sm, self.r_cst], writes=[r_b])
            S.op("dve", lambda e: e.tensor_copy(acum[:BS, 0:n32], bank[:BS, 0:n32]), reads=[r_b], writes=[r_sm])
            S.op("dve", lambda e: e.tensor_copy(tot[:, 0:n32], bank[:, 256:256 + n32]), reads=[r_b], writes=[r_sm])
            S.op("act", lambda e: e.activation(ea[:BS, 0:n32], acum[:BS, 0:n32], AF.Exp), reads=[r_sm], writes=[r_sm])
            S.op("act", lambda e: e.activation(dtot[:, 0:n32], tot[:, 0:n32], AF.Exp), reads=[r_sm], writes=[r_sm])
            S.op("dve", lambda e: e.tensor_tensor(wend[:BS, 0:n32], tot[:BS, 0:n32], acum[:BS, 0:n32], ALU.subtract), reads=[r_sm], writes=[r_sm])
            S.op("act", lambda e: e.activation(wend[:BS, 0:n32], wend[:BS, 0:n32], AF.Exp), reads=[r_sm], writes=[r_sm])
            S.op("dve", lambda e: e.tensor_tensor(wend[:BS, 0:n32], wend[:BS, 0:n32], dt[:BS, 0:n32], ALU.mult), reads=[r_sm], writes=[r_sm])
            DELAY = 2
            pendB = []

            def stageB(ch, src, r_dst):
                bank, r_b = self.rot.next()
                bb = bank.bitcast(BF16)
                for tb in range(NB):
                    S.op("pe", lambda e: e.transpose(bb[:BS, tb * 128:(tb + 1) * 128], src[:, tb * BS:(tb + 1) * BS], self.ident_b),
                         reads=[r_dst, self.r_cst], writes=[r_b])
                if ch < 16:
                    S.op("dve", lambda e: e.tensor_copy(x_tok[:BS, 0:NB, ch * 128:(ch + 1) * 128], bb[:BS, 0:NB * 128].rearrange("p (b c) -> p b c", b=NB)),
                         reads=[r_b], writes=[r_xtok])
                else:
                    g = ch - 16
                    S.op("dve", lambda e: e.tensor_copy(B_tok[:BS, 0:NB, g * 128:(g + 1) * 128], bb[:BS, 0:NB * 128].rearrange("p (b c) -> p b c", b=NB)),
                         reads=[r_b], writes=[r_Btok])

            for j in range(6):
                w, r_w = self.wget("xbc%d" % j)
                for cc in range(4):
                    ch = j * 4 + cc
                    bank, r_b = self.proj_feat(w, r_w, cc * 128, hT, TT, [self.r_hT])
                    x_, r_x = xp.next()
                    S.op("act", lambda e: e.copy(x_[:, 3:3 + TT], bank[:, 0:TT]), reads=[r_b], writes=[r_x])
                    S.op("pool", lambda e: e.tensor_copy(x_[:, 0:3], self.convhist[:, ch, :]), reads=[self.r_convhist], writes=[r_x])
                    a_, r_a = cacc.next()
                    S.op("dve", lambda e: e.tensor_scalar(a_[:, 0:TT], x_[:, 0:TT], self.wconv_c[:, ch * 4:ch * 4 + 1], self.bconv_c[:, ch:ch + 1], ALU.mult, ALU.add),
                         reads=[r_x, self.r_cst], writes=[r_a])
                    for i in range(1, 4):
                        S.op("dve", lambda e: e.scalar_tensor_tensor(a_[:, 0:TT], x_[:, i:i + TT], self.wconv_c[:, ch * 4 + i:ch * 4 + i + 1], a_[:, 0:TT], ALU.mult, ALU.add),
                             reads=[r_x, r_a, self.r_cst], writes=[r_a])
                    S.op("pool", lambda e: e.tensor_copy(self.convhist[:, ch, :], x_[:, TT:TT + 3]), reads=[r_x], writes=[self.r_convhist])
                    if ch < 16:
                        c_, r_c = xc.next()
                        dst, r_dst, src = c_[:, 0:TT], r_c, c_
                    elif ch < 20:
                        dst, r_dst, src = BT[:, ch - 16, 0:TT], r_BT, BT[:, ch - 16, :]
                    else:
                        dst, r_dst, src = CT[:, ch - 20, 0:TT], r_CT, CT[:, ch - 20, :]
                    S.op("act", lambda e: e.activation(dst, a_[:, 0:TT], AF.Silu), reads=[r_a], writes=[r_dst])
                    if ch < 20:
                        pendB.append((ch, src, r_dst))
                    while pendB and pendB[0][0] <= ch - DELAY:
                        stageB(*pendB.pop(0))
            while pendB:
                stageB(*pendB.pop(0))
            items = [(tb, g) for tb in range(NB) for g in range(4)]
            st1 = {}

            def block_prep(tb):
                t0, t1 = tb * BS, (tb + 1) * BS
                h0 = tb * 32
                S.op("dve", lambda e: e.tensor_tensor(xdt[:BS, :].rearrange("p (h d) -> p h d", h=32), x_tok[:BS, tb, :].rearrange("p (h d) -> p h d", h=32),
                                                      dt[:BS, h0:h0 + 32].unsqueeze(2).to_broadcast([BS, 32, 64]), ALU.mult),
                     reads=[r_xtok, r_sm], writes=[r_xdt])
                S.op("pool", lambda e: e.tensor_tensor(xw[:BS, :].rearrange("p (h d) -> p h d", h=32), x_tok[:BS, tb, :].rearrange("p (h d) -> p h d", h=32),
                                                       wend[:BS, h0:h0 + 32].unsqueeze(2).to_broadcast([BS, 32, 64]), ALU.mult),
                     reads=[r_xtok, r_sm], writes=[r_xw])
                bank, r_b = self.rot.next()
                for g in range(4):
                    S.mm(bank[:BS, g * 128:g * 128 + BS], BT[:, g, t0:t1], CT[:, g, t0:t1], start=True, stop=True, reads=[r_BT, r_CT], writes=[r_b])
                S.op("dve", lambda e: e.tensor_tensor(CBm[:BS, :, 0:BS], bank[:BS, :].rearrange("p (g l) -> p g l", g=4)[:, :, 0:BS],
                                                      self.triU_f[:BS, 0:BS].unsqueeze(1).to_broadcast([BS, 4, BS]), ALU.mult),
                     reads=[r_b, self.r_cst], writes=[r_CBm])

            def stage1(tb, g):
                hh0 = tb * 32 + g * 8
                rhsD, r_rhsD, Et, r_E, MT, r_MT = scanb.next()
                S.op("pool", lambda e: e.tensor_tensor(rhsD[:BS, :, 0:BS], dtA[:BS, hh0:hh0 + 8].unsqueeze(2).to_broadcast([BS, 8, BS]),
                                                       self.triU_f[:BS, 0:BS].unsqueeze(1).to_broadcast([BS, 8, BS]), ALU.mult),
                     reads=[r_sm, self.r_cst], writes=[r_rhsD])
                par = (tb * 4 + g) % 2
                for half in range(2):
                    bD, r_bD = self.banks[4 + 2 * par + half]
                    S.mm(bD[:BS, 0:4 * BS], self.triLs_f[:BS, :BS], rhsD[:BS, half * 4:(half + 1) * 4, 0:BS], start=True, stop=True,
                         reads=[r_rhsD, self.r_cst], writes=[r_bD])
                    S.op("act", lambda e: e.activation(Et[:BS, half * 4:(half + 1) * 4, 0:BS], bD[:BS, 0:4 * BS].rearrange("p (j l) -> p j l", j=4), AF.Exp),
                         reads=[r_bD], writes=[r_E])
                tg2, r_tg2 = tmpg.next()
                S.op("pool", lambda e: e.tensor_tensor(tg2[:BS, :].rearrange("p (j d) -> p j d", j=8),
                                                       x_tok[:BS, tb, g * 512:(g + 1) * 512].rearrange("p (j d) -> p j d", j=8),
                                                       self.hb[:BS, 64 + g * 8:64 + g * 8 + 8].unsqueeze(2).to_broadcast([BS, 8, 64]), ALU.mult),
                     reads=[r_xtok, self.r_cst], writes=[r_tg2])
                st1[(tb, g)] = (Et, r_E, MT, r_MT, tg2, r_tg2)

            def stage2(tb, g):
                t0, t1 = tb * BS, (tb + 1) * BS
                hh0 = tb * 32 + g * 8
                Et, r_E, MT, r_MT, tg2, r_tg2 = st1.pop((tb, g))
                yb, r_yb = yblk[tb % 2]
                S.op("dve", lambda e: e.tensor_tensor(MT[:BS, :, 0:BS], Et[:BS, :, 0:BS], CBm[:BS, g, 0:BS].unsqueeze(1).to_broadcast([BS, 8, BS]), ALU.mult),
                     reads=[r_E, r_CBm], writes=[r_MT])
                byd, r_byd = self.rot.next()
                for j in range(8):
                    hh = g * 8 + j
                    S.mm(byd[:BS, j * 64:(j + 1) * 64], MT[:BS, j, 0:BS], xdt[:BS, hh * 64:(hh + 1) * 64], start=True, stop=True,
                         reads=[r_MT, r_xdt], writes=[r_byd])
                byo, r_byo = self.rot.next()
                S.mm(byo[:BS, :], CT[:, g, t0:t1], self.hinT[:, g * 512:(g + 1) * 512], start=True, stop=True,
                     reads=[r_CT, self.r_hinT], writes=[r_byo])
                tg, r_tg = tmpg.next()
                S.op("dve", lambda e: e.tensor_tensor(tg[:BS, :].rearrange("p (j d) -> p j d", j=8), byo[:BS, :].rearrange("p (j d) -> p j d", j=8),
                                                      ea[:BS, hh0:hh0 + 8].unsqueeze(2).to_broadcast([BS, 8, 64]), ALU.mult),
                     reads=[r_byo, r_sm], writes=[r_tg])
                S.op("dve", lambda e: e.tensor_tensor(tg[:BS, :], tg[:BS, :], byd[:BS, :], ALU.add), reads=[r_tg, r_byd], writes=[r_tg])
                S.op("pool", lambda e: e.tensor_tensor(tg[:BS, :], tg[:BS, :], tg2[:BS, :], ALU.add), reads=[r_tg, r_tg2], writes=[r_tg])
                S.op("dve", lambda e: e.tensor_tensor(yb[:BS, g * 512:(g + 1) * 512], tg[:BS, :], zs[:BS, tb, g * 512:(g + 1) * 512], ALU.mult),
                     reads=[r_tg, r_zs], writes=[r_yb])

            def state_update(tb):
                h0 = tb * 32
                for g in range(4):
                    hh0 = h0 + g * 8
                    bst, r_bst = self.rot.next()
                    S.mm(bst[:, :], B_tok[:BS, tb, g * 128:(g + 1) * 128], xw[:BS, g * 512:(g + 1) * 512], start=True, stop=True,
                         reads=[r_Btok, r_xw], writes=[r_bst])
                    S.op("dve", lambda e: e.tensor_tensor(self.stateT[:, g * 512:(g + 1) * 512].rearrange("p (j d) -> p j d", j=8),
                                                          self.stateT[:, g * 512:(g + 1) * 512].rearrange("p (j d) -> p j d", j=8),
                                                          dtot[:, hh0:hh0 + 8].unsqueeze(2).to_broadcast([128, 8, 64]), ALU.mult),
                         reads=[self.r_state, r_sm], writes=[self.r_state])
                    S.op("dve", lambda e: e.tensor_tensor(self.stateT[:, g * 512:(g + 1) * 512], self.stateT[:, g * 512:(g + 1) * 512], bst[:, :], ALU.add),
                         reads=[self.r_state, r_bst], writes=[self.r_state])
                S.op("act", lambda e: e.copy(self.hinT[:], self.stateT[:]), reads=[self.r_state], writes=[self.r_hinT])

            def epilogue(tb):
                t0, t1 = tb * BS, (tb + 1) * BS
                yb, r_yb = yblk[tb % 2]
                S.op("act", lambda e: e.activation(yn[:BS, :], yb[:BS, :], AF.Square, accum_out=st4[:BS, 0:1]), reads=[r_yb], writes=[r_yn, r_st4])
                S.op("act", lambda e: e.activation(st4[:BS, 1:2], st4[:BS, 0:1], AF.Sqrt, bias=self.cc[:BS, 0:1], scale=1.0 / D), reads=[r_st4, self.r_cst], writes=[r_st4])
                S.op("dve", lambda e: e.reciprocal(st4[:BS, 2:3], st4[:BS, 1:2]), reads=[r_st4], writes=[r_st4])
                S.op("act", lambda e: e.activation(yn[:BS, :], yb[:BS, :], AF.Identity, scale=st4[:BS, 2:3]), reads=[r_yb, r_st4], writes=[r_yn])
                for q in range(4):
                    bank, r_b = self.rot.next()
                    bb = bank.bitcast(BF16)
                    for k in range(4):
                        fc = q * 4 + k
                        S.op("pe", lambda e: e.transpose(bb[:, k * BS:(k + 1) * BS], yn[:BS, fc * 128:(fc + 1) * 128], self.ident_b[:BS, :BS]),
                             reads=[r_yn, self.r_cst], writes=[r_b])
                    S.op("dve", lambda e: e.tensor_tensor(yT[:, q * 4:(q + 1) * 4, t0:t1], bb[:, 0:4 * BS].rearrange("p (k b) -> p k b", k=4),
                                                          self.gcols[:, 16 + q * 4:16 + (q + 1) * 4].unsqueeze(2).to_broadcast([128, 4, BS]), ALU.mult),
                         reads=[r_b, self.r_cst], writes=[r_yT])

            block_prep(0)
            stage1(*items[0])
            pend_epi = []
            for i, (tb, g) in enumerate(items):
                if i + 1 < len(items):
                    stage1(*items[i + 1])
                stage2(tb, g)
                if g == 1 and pend_epi:
                    epilogue(pend_epi.pop(0))
                if g == 3:
                    state_update(tb)
                    pend_epi.append(tb)
                    if tb + 1 < NB:
                        block_prep(tb + 1)
            while pend_epi:
                epilogue(pend_epi.pop(0))
            self.branch_project(ph, yT, r_yT, "wos", "gs", True, TT)
        S.phase_end()

    def tile_fox(self, TT, BS, NB, blk0, fk_out, fv_out):
        S = self.S
        hT = self.hT
        scale = 128.0 ** -0.5
        nhist = blk0
        with ExitStack() as pst:
            sb = lambda name, shape, dt: self.sb(pst, name, shape, dt)
            ph = {"sig": Rot([(sb("f_sig%d" % i, [128, 512], F32), S.pres("f_sig")) for i in range(2)])}
            yT = sb("f_yT", [128, 16, 512], BF16); r_yT = S.pres("f_yT")
            V_cur = sb("f_Vcur", [128, 4, 2048], BF16); r_Vcur = S.pres("f_Vcur")
            qT = sb("f_qT", [128, 4, 512], BF16); r_qT = S.pres("f_qT")
            kT = sb("f_kT", [128, 4, 512], BF16); r_kT = S.pres("f_kT")
            gT = sb("f_gT", [128, 4, 512], BF16); r_gT = S.pres("f_gT")
            stg = Rot([(sb("f_stg%d" % i, [128, 512], F32), S.pres("f_stg")) for i in range(3)])
            khb = Rot([(sb("f_khb%d" % i, [128, 1536], BF16), S.pres("f_khb")) for i in range(2)])
            vhb = Rot([(sb("f_vhb%d" % i, [128, 12, 128], BF16), S.pres("f_vhb")) for i in range(2)])
            PT = Rot([(sb("f_PT%d" % i, [128, 512], BF16), S.pres("f_PT")) for i in range(3)])
            rl = sb("f_rl", [128, 512], F32); r_rl = S.pres("f_rl")
            ctb = sb("f_ctb", [128, 4, 16], BF16); r_ctb = S.pres("f_ctb")
            S.op("dve", lambda e: e.tensor_scalar(ctb[:BS, 0:NB, :], self.negc[:BS, blk0:blk0 + NB, :], -(128.0 ** 0.5), None, ALU.mult),
                 reads=[self.r_negc], writes=[r_ctb])
            bank, r_b = self.rot.next()
            bb = bank.bitcast(BF16)
            for tb in range(NB):
                S.op("pe", lambda e: e.transpose(bb[:16, tb * BS:(tb + 1) * BS], ctb[:BS, tb, :], self.ident_b[:BS, :BS]),
                     reads=[r_ctb, self.r_cst], writes=[r_b])
            S.op("dve", lambda e: e.tensor_copy(self.cT_bf[:, 0:TT], bb[:16, 0:TT]), reads=[r_b], writes=[self.r_cT])

            for hg in range(4):
                w, r_w = self.wget("fq%d" % hg)
                for cc in range(4):
                    bank, r_b = self.proj_feat(w, r_w, cc * 128, hT, TT, [self.r_hT])
                    S.op("act", lambda e: e.copy(qT[:, cc, 0:TT], bank[:, 0:TT]), reads=[r_b], writes=[r_qT])
                w, r_w = self.wget("fk%d" % hg)
                for cc in range(4):
                    bank, r_b = self.proj_feat(w, r_w, cc * 128, hT, TT, [self.r_hT])
                    S.op("dve", lambda e: e.tensor_copy(kT[:, cc, 0:TT], bank[:, 0:TT]), reads=[r_b], writes=[r_kT])
                for tb in range(NB):
                    bank, r_b = self.proj_tok(hT, tb * BS, BS, w, r_w, 512, [self.r_hT])
                    sg, r_sg = stg.next()
                    S.op("act", lambda e: e.copy(sg[:BS, :], bank[:BS, :]), reads=[r_b], writes=[r_sg])
                    S.dma("act", fk_out[tb * BS:(tb + 1) * BS, hg * 512:(hg + 1) * 512], sg[:BS, :], reads=[r_sg])
                for cc in range(4):
                    h = hg * 4 + cc
                    S.dma("act", self.kT_hist[h, :, blk0 * 128:blk0 * 128 + TT], kT[:, cc, 0:TT], reads=[r_kT], writes=[self.r_khist])
                w, r_w = self.wget("fv%d" % hg)
                for tb in range(NB):
                    bank, r_b = self.proj_tok(hT, tb * BS, BS, w, r_w, 512, [self.r_hT])
                    sg, r_sg = stg.next()
                    S.op("act", lambda e: e.copy(sg[:BS, :], bank[:BS, :]), reads=[r_b], writes=[r_sg])
                    S.dma("act", fv_out[tb * BS:(tb + 1) * BS, hg * 512:(hg + 1) * 512], sg[:BS, :], reads=[r_sg])
                    S.op("dve", lambda e: e.tensor_copy(V_cur[:BS, tb, hg * 512:(hg + 1) * 512], bank[:BS, :]), reads=[r_b], writes=[r_Vcur])
                    if BS == 128:
                        S.dma("act", self.v_hist[hg * 4:(hg + 1) * 4, :, blk0 + tb, :].rearrange("h p d -> p h d"),
                              V_cur[:, tb, hg * 512:(hg + 1) * 512].rearrange("p (h d) -> p h d", h=4), reads=[r_Vcur], writes=[self.r_vhist])
                w, r_w = self.wget("fg%d" % hg)
                for cc in range(4):
                    bank, r_b = self.proj_feat(w, r_w, cc * 128, hT, TT, [self.r_hT])
                    S.op("act", lambda e: e.activation(gT[:, cc, 0:TT], bank[:, 0:TT], AF.Silu), reads=[r_b], writes=[r_gT])
                nk = nhist + NB
                items = [(cc, j) for cc in range(4) for j in range(nk)]
                hist = {}
                pend = {}

                def emit_scores(cc, j):
                    h = hg * 4 + cc
                    if j == 0 and nhist > 0:
                        kh, r_kh = khb.next()
                        vh, r_vh = vhb.next()
                        S.dma("sp", kh[:, 0:nhist * 128], self.kT_hist[h, :, 0:nhist * 128], reads=[self.r_khist], writes=[r_kh])
                        S.dma("sp", vh[:, 0:nhist, :], self.v_hist[h, :, 0:nhist, :], reads=[self.r_vhist], writes=[r_vh])
                        hist[cc] = (kh, r_kh, vh, r_vh)
                    if j < nhist:
                        kh, r_kh, vh, r_vh = hist[cc]
                        kb = 128
                        k_ap, r_k = kh[:, j * 128:(j + 1) * 128], r_kh
                        v_ap, r_v = vh[:, j, :], r_vh
                        q0 = 0
                        diag = False
                    else:
                        tb = j - nhist
                        kb = BS
                        k_ap, r_k = kT[:, cc, tb * BS:(tb + 1) * BS], r_kT
                        v_ap, r_v = V_cur[:BS, tb, h * 128:(h + 1) * 128], r_Vcur
                        q0 = tb * BS
                        diag = True
                    nq = TT - q0
                    bS, r_bS = self.rot.next()
                    S.mm(bS[:kb, 0:nq], k_ap, qT[:, cc, q0:TT], start=True, stop=False, reads=[r_k, r_qT], writes=[r_bS])
                    S.mm(bS[:kb, 0:nq], self.sel_b[:, h * 128:h * 128 + kb], self.cT_bf[:, q0:TT], start=False, stop=True,
                         reads=[self.r_cst, self.r_cT], writes=[r_bS])
                    pend[(cc, j)] = (bS, r_bS, kb, q0, nq, diag, v_ap, r_v)

                def emit_pv(cc, j):
                    h = hg * 4 + cc
                    bS, r_bS, kb, q0, nq, diag, v_ap, r_v = pend.pop((cc, j))
                    bO, r_bO = self.banks[4 + (cc % 2) * 2]
                    bL, r_bL = self.banks[5 + (cc % 2) * 2]
                    p_, r_p = PT.next()
                    S.op("act", lambda e: e.activation(p_[:kb, 0:nq], bS[:kb, 0:nq], AF.Exp, bias=self.negc[:kb, j, h:h + 1], scale=scale),
                         reads=[r_bS, self.r_negc], writes=[r_p])
                    if diag:
                        S.op("pool", lambda e: e.tensor_tensor(p_[:kb, 0:BS], p_[:kb, 0:BS], self.triU_b[:kb, 0:BS], ALU.mult),
                             reads=[r_p, self.r_cst], writes=[r_p])
                    S.mm(bO[:, q0:TT], v_ap, p_[:kb, 0:nq], start=(j == 0), stop=(j == nk - 1), reads=[r_v, r_p], writes=[r_bO])
                    S.mm(bL[:, q0:TT], self.ones_b[:kb, :], p_[:kb, 0:nq], start=(j == 0), stop=(j == nk - 1), reads=[self.r_cst, r_p], writes=[r_bL])
                    if j == nk - 1:
                        S.op("dve", lambda e: e.reciprocal(rl[:, 0:TT], bL[:, 0:TT]), reads=[r_bL], writes=[r_rl])
                        S.op("dve", lambda e: e.tensor_tensor(rl[:, 0:TT], rl[:, 0:TT], gT[:, cc, 0:TT], ALU.mult), reads=[r_rl, r_gT], writes=[r_rl])
                        S.op("dve", lambda e: e.tensor_tensor(yT[:, h, 0:TT], bO[:, 0:TT], rl[:, 0:TT], ALU.mult), reads=[r_bO, r_rl], writes=[r_yT])

                emit_scores(*items[0])
                for i, it in enumerate(items):
                    if i + 1 < len(items):
                        emit_scores(*items[i + 1])
                    emit_pv(*it)
            self.branch_project(ph, yT, r_yT, "wof", "gf", False, TT)
        S.phase_end()

    def tile_mem(self, TT, BS, NB, nxt=None):
        S = self.S
        hT = self.hT
        scale = 512.0 ** -0.5
        with ExitStack() as pst:
            sb = lambda name, shape, dt: self.sb(pst, name, shape, dt)
            ph = {"sig": Rot([(sb("m_sig%d" % i, [128, 512], F32), S.pres("m_sig")) for i in range(2)])}
            yT = sb("m_yT", [128, 16, 512], BF16); r_yT = S.pres("m_yT")
            memKT = sb("m_memKT", [128, 16, 256], BF16); r_memKT = S.pres("m_memKT")
            memV = sb("m_memV", [128, 2, 2048], BF16); r_memV = S.pres("m_memV")
            mqT = sb("m_mqT", [128, 4, 512], BF16); r_mqT = S.pres("m_mqT")
            mgT = sb("m_mgT", [128, 4, 512], BF16); r_mgT = S.pres("m_mgT")
            PT = [(sb("m_PT%d" % i, [128, 512], BF16), S.pres("m_PT")) for i in range(2)]
            rl = sb("m_rl", [128, 512], F32); r_rl = S.pres("m_rl")
            tmp = Rot([(sb("m_tmp%d" % i, [128, 512], F32), S.pres("m_tmp")) for i in range(2)])
            S.dma("sp", memKT[:], self.memKT_d, reads=[self.r_memKT_d], writes=[r_memKT])
            S.dma("sp", memV[:], self.memV_d, reads=[self.r_memV_d], writes=[r_memV])
            for mh in range(4):
                w, r_w = self.wget("mq%d" % mh)
                for cc in range(4):
                    bank, r_b = self.proj_feat(w, r_w, cc * 128, hT, TT, [self.r_hT])
                    S.op("act", lambda e: e.copy(mqT[:, cc, 0:TT], bank[:, 0:TT]), reads=[r_b], writes=[r_mqT])
                w, r_w = self.wget("mg%d" % mh)
                for cc in range(4):
                    bank, r_b = self.proj_feat(w, r_w, cc * 128, hT, TT, [self.r_hT])
                    S.op("act", lambda e: e.activation(mgT[:, cc, 0:TT], bank[:, 0:TT], AF.Silu), reads=[r_b], writes=[r_mgT])
                for mb in range(2):
                    bS, r_bS = self.rot.next()
                    for dc in range(4):
                        S.mm(bS[:, 0:TT], memKT[:, mh * 4 + dc, mb * 128:(mb + 1) * 128], mqT[:, dc, 0:TT], start=(dc == 0), stop=(dc == 3),
                             reads=[r_memKT, r_mqT], writes=[r_bS])
                    p_, r_p = PT[mb]
                    S.op("act", lambda e: e.activation(p_[:, 0:TT], bS[:, 0:TT], AF.Exp, scale=scale), reads=[r_bS], writes=[r_p])
                bL, r_bL = self.banks[4]
                for mb in range(2):
                    S.mm(bL[:, 0:TT], self.ones_b, PT[mb][0][:, 0:TT], start=(mb == 0), stop=(mb == 1), reads=[self.r_cst, PT[mb][1]], writes=[r_bL])
                S.op("dve", lambda e: e.reciprocal(rl[:, 0:TT], bL[:, 0:TT]), reads=[r_bL], writes=[r_rl])
                for dc in range(4):
                    bO, r_bO = self.banks[5 + (dc % 3)]
                    for mb in range(2):
                        S.mm(bO[:, 0:TT], memV[:, mb, (mh * 4 + dc) * 128:(mh * 4 + dc + 1) * 128], PT[mb][0][:, 0:TT], start=(mb == 0), stop=(mb == 1),
                             reads=[r_memV, PT[mb][1]], writes=[r_bO])
                    t_, r_t = tmp.next()
                    S.op("dve", lambda e: e.tensor_tensor(t_[:, 0:TT], rl[:, 0:TT], mgT[:, dc, 0:TT], ALU.mult), reads=[r_rl, r_mgT], writes=[r_t])
                    S.op("dve", lambda e: e.tensor_tensor(yT[:, mh * 4 + dc, 0:TT], bO[:, 0:TT], t_[:, 0:TT], ALU.mult), reads=[r_bO, r_t], writes=[r_yT])
            pre = []
            if nxt is not None:
                BSn = nxt["BS"]
                NBn = nxt["TT"] // BSn
                t1ph = {
                    "xs": Rot([(sb("m_t1xs%d" % i, [128, 2048], F32), S.pres("m_t1xs")) for i in range(2)]),
                    "junk": (sb("m_t1junk", [128, 2048], BF16), S.pres("m_t1junk")),
                    "small": Rot([(sb("m_t1sm%d" % i, [128, 4], F32), S.pres("m_t1sm")) for i in range(2)]),
                }
                for tb in range(min(2, NBn)):
                    pre.append(self.norm_load(t1ph, nxt["x_src"][tb * BSn:(tb + 1) * BSn, :], BSn, q="act"))
            self.branch_project(ph, yT, r_yT, "wom", "gm", False, TT)
            if nxt is not None:
                for tb, (xs_, r_xs_) in enumerate(pre):
                    self.norm_tr(xs_, r_xs_, BSn, self.hT, tb * BSn, self.gcols[:, 0:16], self.r_hT)
                for tb in range(2, NBn):
                    xs_, r_xs_ = self.norm_load(t1ph, nxt["x_src"][tb * BSn:(tb + 1) * BSn, :], BSn, q="act")
                    self.norm_tr(xs_, r_xs_, BSn, self.hT, tb * BSn, self.gcols[:, 0:16], self.r_hT)
        S.phase_end()

    def tile_final(self, TT, BS, NB, x_src, y_out):
        S = self.S
        with ExitStack() as pst:
            sb = lambda name, shape, dt: self.sb(pst, name, shape, dt)
            xs = sb("o_xs", [128, 4, 2048], F32); r_xs = [S.pres("o_xs%d" % i) for i in range(4)]
            xo = sb("o_xo", [128, 4, 2048], F32); r_xo = [S.pres("o_xo%d" % i) for i in range(4)]
            gfb = sb("o_gfb", [128, 2048], F32); r_gfb = S.pres("o_gfb")
            junk = sb("o_junk", [128, 2048], BF16); r_junk = S.pres("o_junk")
            st4 = [(sb("o_st%d" % i, [128, 4], F32), S.pres("o_st")) for i in range(4)]
            for tb in range(NB):
                S.dma("act", xs[:BS, tb, :], x_src[tb * BS:(tb + 1) * BS, :], writes=[r_xs[tb]])
            S.dma("act", gfb[:], self.I["g_final"].partition_broadcast(128), writes=[r_gfb])
            for j in range(4):
                w, r_w = self.wget("wout%d" % j)
                for tb in range(NB):
                    bank, r_b = self.proj_tok(self.mergedT, tb * BS, BS, w, r_w, 512, [self.r_merged])
                    S.op("dve", lambda e: e.tensor_tensor(xo[:BS, tb, j * 512:(j + 1) * 512], bank[:BS, :], xs[:BS, tb, j * 512:(j + 1) * 512], ALU.add),
                         reads=[r_b, r_xs[tb]], writes=[r_xo[tb]])
            for tb in range(NB):
                s4, r_s4 = st4[tb]
                S.op("act", lambda e: e.activation(junk[:BS, :], xo[:BS, tb, :], AF.Square, accum_out=s4[:BS, 0:1]), reads=[r_xo[tb]], writes=[r_junk, r_s4])
                S.op("act", lambda e: e.activation(s4[:BS, 1:2], s4[:BS, 0:1], AF.Sqrt, bias=self.cc[:BS, 0:1], scale=1.0 / D), reads=[r_s4, self.r_cst], writes=[r_s4])
                S.op("dve", lambda e: e.reciprocal(s4[:BS, 2:3], s4[:BS, 1:2]), reads=[r_s4], writes=[r_s4])
                S.op("dve", lambda e: e.scalar_tensor_tensor(xo[:BS, tb, :], xo[:BS, tb, :], s4[:BS, 2:3], gfb[:BS, :], ALU.mult, ALU.mult),
                     reads=[r_xo[tb], r_s4, r_gfb], writes=[r_xo[tb]])
                S.dma("act", y_out[tb * BS:(tb + 1) * BS, :], xo[:BS, tb, :], reads=[r_xo[tb]])
        S.phase_end()


_CACHE = {}


def _consts():
    ident = np.eye(128, dtype=np.float32)
    s = np.arange(128)
    triU = (s[:, None] <= s[None, :]).astype(np.float32)
    triLs = (s[:, None] > s[None, :]).astype(np.float32)
    ones = np.ones((128, 128), np.float32)
    cst = np.concatenate([ident, triU, triLs, ones], axis=1)
    sel = np.zeros((16, 16, 128), np.float32)
    for h in range(16):
        sel[h, h, :] = 1.0
    return cst, sel.reshape(16, 2048)


def kernel(x_prompt, x_sample, mem_prompt, cache_fox_k, cache_fox_v, cache_fox_logf, state_ssd, state_ssd_conv,
           cache_mem_k, cache_mem_v, g_norm, w_in, w_conv, b_conv, dt_bias, a_log, d_skip, g_ssd_out, b_forget,
           g_mem, w_mem_kv, w_o_ssd, w_o_fox, w_o_mem, w_out, g_final):
    f = lambda a: np.ascontiguousarray(np.asarray(a, dtype=np.float32))
    if "nc" not in _CACHE:
        _CACHE["nc"] = Prog().build()
    nc = _CACHE["nc"]
    cst, sel = _consts()
    gcols = np.concatenate([f(g_norm)[0].reshape(16, 128).T, f(g_ssd_out)[0].reshape(16, 128).T,
                            f(g_mem)[0].reshape(16, 128).T], axis=1)
    wconv_c = f(w_conv)[0].reshape(4, 24, 128).transpose(2, 1, 0).reshape(128, 96)
    bconv_c = f(b_conv)[0].reshape(24, 128).T
    hb = np.concatenate([np.broadcast_to(f(dt_bias)[0][None, :], (128, 32)), np.broadcast_to(f(a_log)[0][None, :], (128, 32)),
                         np.broadcast_to(f(d_skip)[0][None, :], (128, 32)), np.broadcast_to(f(b_forget)[0][None, :], (128, 16))], axis=1)
    shared = {
        "w_in": f(w_in)[0], "w_mem_kv": f(w_mem_kv)[0], "w_o_ssd": f(w_o_ssd)[0], "w_o_fox": f(w_o_fox)[0],
        "w_o_mem": f(w_o_mem)[0], "w_out": f(w_out)[0], "g_final": f(g_final),
        "gcols": f(gcols), "wconv_c": f(wconv_c), "bconv_c": f(bconv_c), "hb": f(hb), "cst": cst, "selc": sel,
    }
    xp = f(x_prompt); xs = f(x_sample); mp = f(mem_prompt)
    in_maps = []
    for c in range(8):
        m = dict(shared)
        m["xp"] = xp[2 * c:2 * c + 2]
        m["xs"] = xs[c]
        m["memp"] = mp[2 * c:2 * c + 2]
        m["ck"] = f(cache_fox_k)[0, c].reshape(1024, 2048)
        m["cv"] = f(cache_fox_v)[0, c].reshape(1024, 2048)
        m["clf"] = f(cache_fox_logf)[0, c]
        m["sst"] = f(state_ssd)[0, c].reshape(2048, 128)
        m["sconv"] = f(state_ssd_conv)[0, c]
        m["cmk"] = f(cache_mem_k)[0, c].reshape(256, 2048)
        m["cmv"] = f(cache_mem_v)[0, c].reshape(256, 2048)
        in_maps.append(m)
    res = run_bass_kernel_spmd(nc, in_maps, core_ids=list(range(8)))
    R = res.results
    cat = lambda k: np.concatenate([np.asarray(r[k]) for r in R], axis=0)
    stk = lambda k: np.stack([np.asarray(r[k]) for r in R], axis=0)
    y_prompt = cat("yp")
    y_sample = stk("ys")
    fkp = cat("fkp").reshape(1, 16, 2048, 16, 128)
    fvp = cat("fvp").reshape(1, 16, 2048, 16, 128)
    flp = cat("flp").reshape(1, 16, 2048, 16)
    ssdp = cat("ssdp").reshape(1, 16, 32, 64, 128)
    convp = cat("convp").reshape(1, 16, 3, 3072)
    mkp = cat("mkp").reshape(1, 16, 256, 4, 512)
    mvp = cat("mvp").reshape(1, 16, 256, 4, 512)
    fks = stk("fks").reshape(1, 8, 64, 16, 128)
    fvs = stk("fvs").reshape(1, 8, 64, 16, 128)
    fls = stk("fls").reshape(1, 8, 64, 16)
    ssds = stk("ssds").reshape(1, 8, 32, 64, 128)
    convs = stk("convs").reshape(1, 8, 3, 3072)
    return (y_prompt, y_sample, fkp, fvp, flp, ssdp, convp, mkp, mvp, fks, fvs, fls, ssds, convs)
```

```python
import numpy as np
import concourse.bass as bass
import concourse.mybir as mybir
from concourse.bass_utils import run_bass_kernel_spmd
from contextlib import ExitStack

F32 = mybir.dt.float32
BF16 = mybir.dt.bfloat16
AF = mybir.ActivationFunctionType
ALU = mybir.AluOpType

import os as _os
NDS = int(_os.environ.get('KNDS', '22'))
NONESHOT = 73
D = 2048
EPS = 1e-6
INW = 23600
SPL = dict(z=0, xbc=2048, dt=5120, fq=5152, fk=7200, fv=9248, fg=11296, ff=13344, mq=13360, mg=15408,
           gs=17456, gf=19504, gm=21552)


class Res:
    __slots__ = ("name", "w", "r", "excl")

    def __init__(self, name, excl=False):
        self.name = name
        self.w = None
        self.r = []
        self.excl = excl


class Rot:
    def __init__(self, items):
        self.items = items
        self.i = 0

    def next(self):
        it = self.items[self.i]
        self.i = (self.i + 1) % len(self.items)
        return it


class Sched:
    def __init__(self, nc, stack, same_engine_sync=True):
        self.nc = nc
        self.engs = {"pe": nc.tensor, "act": nc.scalar, "dve": nc.vector, "pool": nc.gpsimd, "sp": nc.sync}
        self.sem = {k: stack.enter_context(nc.semaphore(k + "_s")) for k in ["pe", "act", "dve", "pool"]}
        self.cnt = {k: 0 for k in self.sem}
        self.seen = {k: {} for k in self.engs}
        self._dummy = [stack.enter_context(nc.semaphore("dummy%d" % i)) for i in range(int(_os.environ.get("KDUMMY", "0")))]
        self.dsems = [stack.enter_context(nc.semaphore("dq%d" % i)) for i in range(NDS + NONESHOT)]
        self.dval = [0] * (NDS + NONESHOT)
        self.dnext = 0
        self.oneshot_next = NDS
        self.same_engine_sync = same_engine_sync
        self.ninst = 0
        self.nwait = 0
        self.live_ranges = []
        self.last_range = None
        self.phase_res = []
        self.halted = False

    def pres(self, name):
        if self.last_range is None:
            lo, hi = self.prev_range
        else:
            lo, hi = self.last_range
        self.prev_range = (lo, hi)
        self.last_range = None
        r = Res(name)
        toks = {}
        for (l2, h2, tk) in self.live_ranges:
            if l2 < hi and lo < h2:
                for (k, v) in tk:
                    if toks.get(k, 0) < v:
                        toks[k] = v
        r.r = list(toks.items())
        self.phase_res.append((r, lo, hi))
        return r

    def phase_end(self):
        new = []
        for (r, lo, hi) in self.phase_res:
            toks = {}
            for (k, v) in ([r.w] if r.w is not None else []) + r.r:
                if toks.get(k, 0) < v:
                    toks[k] = v
            new.append((lo, hi, list(toks.items())))
        old = [e for e in self.live_ranges if not any(lo <= e[0] and e[1] <= hi for (lo, hi, _) in new)]
        self.live_ranges = old + new
        self.phase_res = []

    def _semobj(self, key):
        if isinstance(key, str):
            return self.sem[key]
        return self.dsems[key[1]]

    def _wait(self, eng, deps):
        seen = self.seen[eng]
        best = {}
        for (k, v) in deps:
            if best.get(k, 0) < v:
                best[k] = v
        for k, v in best.items():
            if k == eng:
                if eng == "pe" or not self.same_engine_sync:
                    continue
            if seen.get(k, 0) >= v:
                continue
            self.engs[eng].wait_ge(self._semobj(k), v)
            seen[k] = v
            self.nwait += 1

    def _deps(self, reads, writes, eng=None):
        deps = []
        for b in reads:
            if b.w is not None:
                deps.append(b.w)
            if b.excl:
                deps.extend(t for t in b.r if t[0] != eng)
        for b in writes:
            if b.w is not None:
                deps.append(b.w)
            deps.extend(b.r)
        return deps

    def _record(self, tok, reads, writes):
        for b in reads:
            if len(b.r) > 24:
                best = {}
                for (k, v) in b.r:
                    if best.get(k, 0) < v:
                        best[k] = v
                b.r = list(best.items())
            b.r.append(tok)
        for b in writes:
            b.w = tok
            b.r = []

    def op(self, eng, fn, reads=(), writes=()):
        if self.halted:
            return None
        self._wait(eng, self._deps(reads, writes, eng))
        inst = fn(self.engs[eng])
        self.cnt[eng] += 1
        inst.then_inc(self.sem[eng], 1)
        self._record((eng, self.cnt[eng]), reads, writes)
        self.ninst += 1
        return inst

    def mm(self, out, lhsT, rhs, start, stop, reads=(), writes=()):
        return self.op("pe", lambda e: e.matmul(out, lhsT, rhs, start=start, stop=stop), reads, writes)

    def dma(self, q, out, in_, reads=(), writes=(), **kw):
        if self.halted:
            return None
        deps = self._deps(reads, writes)
        if q == "pool":
            i = self.oneshot_next
            self.oneshot_next += 1
            assert i < NDS + NONESHOT, "out of one-shot DMA semaphores"
        else:
            i = self.dnext
            self.dnext = (self.dnext + 1) % NDS
        key = ("d", i)
        if self.dval[i] > 0:
            deps.append((key, self.dval[i]))
        self._wait(q, deps)
        inst = self.engs[q].dma_start(out=out, in_=in_, **kw)
        self.dval[i] += 16
        inst.then_inc(self.dsems[i], 16)
        tok = (key, self.dval[i])
        self._record(tok, reads, writes)
        self.ninst += 1
        return tok

    def finish(self):
        deps = []
        for i in range(NDS + NONESHOT):
            if self.dval[i] > 0:
                deps.append((("d", i), self.dval[i]))
        for k in self.cnt:
            if self.cnt[k] > 0:
                deps.append((k, self.cnt[k]))
        self._wait("sp", deps)


def weight_chunks():
    ch = {}
    ch["dtff"] = ("w_in", [(SPL["dt"], 32, 0), (SPL["ff"], 16, 32)], 48)
    for j in range(4):
        ch["z%d" % j] = ("w_in", [(SPL["z"] + 512 * j, 512, 0)], 512)
    for j in range(6):
        ch["xbc%d" % j] = ("w_in", [(SPL["xbc"] + 512 * j, 512, 0)], 512)
    for j in range(4):
        ch["wos%d" % j] = ("w_o_ssd", [(512 * j, 512, 0)], 512)
        ch["gs%d" % j] = ("w_in", [(SPL["gs"] + 512 * j, 512, 0)], 512)
    for j in range(4):
        for nm in ("fq", "fk", "fv", "fg"):
            ch["%s%d" % (nm, j)] = ("w_in", [(SPL[nm] + 512 * j, 512, 0)], 512)
    for j in range(4):
        ch["wof%d" % j] = ("w_o_fox", [(512 * j, 512, 0)], 512)
        ch["gf%d" % j] = ("w_in", [(SPL["gf"] + 512 * j, 512, 0)], 512)
    for j in range(4):
        for nm in ("mq", "mg"):
            ch["%s%d" % (nm, j)] = ("w_in", [(SPL[nm] + 512 * j, 512, 0)], 512)
    for j in range(4):
        ch["wom%d" % j] = ("w_o_mem", [(512 * j, 512, 0)], 512)
        ch["gm%d" % j] = ("w_in", [(SPL["gm"] + 512 * j, 512, 0)], 512)
    for j in range(4):
        ch["wout%d" % j] = ("w_out", [(512 * j, 512, 0)], 512)
    return ch


TILE_WORDER = (["dtff"] + ["z%d" % j for j in range(4)] + ["xbc%d" % j for j in range(6)]
               + [n for j in range(4) for n in ("wos%d" % j, "gs%d" % j)]
               + [n for j in range(4) for n in ("fq%d" % j, "fk%d" % j, "fv%d" % j, "fg%d" % j)]
               + [n for j in range(4) for n in ("wof%d" % j, "gf%d" % j)]
               + [n for j in range(4) for n in ("mq%d" % j, "mg%d" % j)]
               + [n for j in range(4) for n in ("wom%d" % j, "gm%d" % j)]
               + ["wout%d" % j for j in range(4)])
MKV_WORDER = ["mkv%d" % j for j in range(8)]


class StopBuild(Exception):
    pass


class Prog:
    def __init__(self, n_prompt_seq=2, prompt_tiles=4, do_sample=True):
        self.n_prompt_seq = n_prompt_seq
        self.prompt_tiles = prompt_tiles
        self.do_sample = do_sample
        self.nc = bass.Bass("TRN2", target_bir_lowering=False)

    def din(self, name, shape, dt=F32):
        return self.nc.dram_tensor(name, list(shape), dt, kind="ExternalInput").ap()

    def dout(self, name, shape, dt=F32):
        return self.nc.dram_tensor(name, list(shape), dt, kind="ExternalOutput").ap()

    def dscr(self, name, shape, dt=BF16):
        return self.nc.dram_tensor(name, list(shape), dt, kind="Internal").ap()

    def sb(self, st, name, shape, dt):
        self._uid = getattr(self, "_uid", 0) + 1
        t = st.enter_context(self.nc.sbuf_tensor("sb%d_%s" % (self._uid, name), list(shape), dt))
        nbytes = int(np.prod(shape[1:])) * (2 if dt == BF16 else 4)
        cur = getattr(self, "_sb_cur", 0)
        off = (cur + 31) // 32 * 32
        self._sb_cur = off + nbytes
        st.callback(self._sb_release, cur)
        if hasattr(self, "S"):
            self.S.last_range = (off - 32, off + nbytes + 32)
        return t

    def _sb_release(self, cur):
        self._sb_cur = cur

    def build(self):
        nc = self.nc
        NP = 2
        I = {}
        I["xp"] = self.din("xp", [NP, 2048, D])
        I["xs"] = self.din("xs", [64, D])
        I["memp"] = self.din("memp", [NP, 256, D])
        I["ck"] = self.din("ck", [1024, D])
        I["cv"] = self.din("cv", [1024, D])
        I["clf"] = self.din("clf", [1024, 16])
        I["sst"] = self.din("sst", [2048, 128])
        I["sconv"] = self.din("sconv", [3, 3072])
        I["cmk"] = self.din("cmk", [256, D])
        I["cmv"] = self.din("cmv", [256, D])
        I["w_in"] = self.din("w_in", [D, INW])
        I["w_mem_kv"] = self.din("w_mem_kv", [D, 4096])
        for n in ("w_o_ssd", "w_o_fox", "w_o_mem", "w_out"):
            I[n] = self.din(n, [D, D])
        I["g_final"] = self.din("g_final", [D])
        I["gcols"] = self.din("gcols", [128, 48])
        I["wconv_c"] = self.din("wconv_c", [128, 24 * 4])
        I["bconv_c"] = self.din("bconv_c", [128, 24])
        I["hb"] = self.din("hb", [128, 112])
        I["cst"] = self.din("cst", [128, 4 * 128])
        I["selc"] = self.din("selc", [16, 16 * 128])
        O = {}
        O["yp"] = self.dout("yp", [NP, 2048, D])
        O["ys"] = self.dout("ys", [64, D])
        O["fkp"] = self.dout("fkp", [NP, 2048, D])
        O["fvp"] = self.dout("fvp", [NP, 2048, D])
        O["flp"] = self.dout("flp", [NP, 2048, 16])
        O["ssdp"] = self.dout("ssdp", [NP, 2048, 128])
        O["convp"] = self.dout("convp", [NP, 3, 3072])
        O["mkp"] = self.dout("mkp", [NP, 256, D])
        O["mvp"] = self.dout("mvp", [NP, 256, D])
        O["fks"] = self.dout("fks", [64, D])
        O["fvs"] = self.dout("fvs", [64, D])
        O["fls"] = self.dout("fls", [64, 16])
        O["ssds"] = self.dout("ssds", [2048, 128])
        O["convs"] = self.dout("convs", [3, 3072])
        self.I, self.O = I, O

        chunks = weight_chunks()
        for j in range(8):
            chunks["mkv%d" % j] = ("w_mem_kv", [(512 * j, 512, 0)], 512)
        self.chunks = chunks
        self.wscr = {n: self.dscr("wb_" + n, [128, 16, c[2]]) for n, c in chunks.items()}
        self.r_wscr = {n: Res("wb_" + n) for n in chunks}
        self.kT_hist = self.dscr("kT_hist", [16, 128, 2048])
        self.v_hist = self.dscr("v_hist", [16, 128, 16, 128])
        self.memKT_d = self.dscr("memKT_d", [128, 16, 256])
        self.memV_d = self.dscr("memV_d", [128, 2, 2048])
        self.r_khist = Res("khist")
        self.r_vhist = Res("vhist")
        self.r_memKT_d = Res("memKT_d")
        self.r_memV_d = Res("memV_d")

        with ExitStack() as st:
            self.S = S = Sched(nc, st, same_engine_sync=(_os.environ.get('KSES', '1') == '1'))
            sb = lambda name, shape, dt: self.sb(st, name, shape, dt)
            self.cst_f = sb("cst_f", [128, 512], F32)
            self.cst_b = sb("cst_b", [128, 512], BF16)
            self.sel_b = sb("sel_b", [16, 2048], BF16)
            self.gcols = sb("gcols", [128, 48], F32)
            self.wconv_c = sb("wconv_c", [128, 96], F32)
            self.bconv_c = sb("bconv_c", [128, 24], F32)
            self.hb = sb("hb", [128, 112], F32)
            self.A_b = sb("A_b", [128, 32], F32)
            self.cc = sb("cc", [128, 4], F32)
            self.r_cst = Res("cst")
            self.ident_f = self.cst_f[:, 0:128]
            self.triU_f = self.cst_f[:, 128:256]
            self.triLs_f = self.cst_f[:, 256:384]
            self.ones_f = self.cst_f[:, 384:512]
            self.ident_b = self.cst_b[:, 0:128]
            self.triU_b = self.cst_b[:, 128:256]
            self.ones_b = self.cst_b[:, 384:512]
            self.stateT = sb("stateT", [128, 2048], F32)
            self.hinT = sb("hinT", [128, 2048], BF16)
            self.convhist = sb("convhist", [128, 24, 3], F32)
            self.negc = sb("negc", [128, 17, 16], F32)
            self.carry = sb("carry", [128, 16], F32)
            self.r_state = Res("state")
            self.r_hinT = Res("hinT")
            self.r_convhist = Res("convhist")
            self.r_negc = Res("negc")
            self.r_carry = Res("carry")
            self.hT = sb("hT", [128, 16, 512], BF16)
            self.mergedT = sb("mergedT", [128, 16, 512], BF16)
            self.lf_tok = sb("lf_tok", [128, 4, 16], F32)
            self.cT_bf = sb("cT_bf", [16, 512], BF16)
            self.r_hT = Res("hT")
            self.r_merged = Res("merged")
            self.r_lf = Res("lf")
            self.r_cT = Res("cT")
            self.wbufs = Rot([(sb("wbuf%d" % i, [128, 16, 512], BF16), Res("wbuf%d" % i)) for i in range(2)])
            self.wbuf_dtff = (sb("wbuf_dtff", [128, 16, 48], BF16), Res("wbuf_dtff"))
            self.ps = st.enter_context(nc.psum_tensor("ps", [128, 4096], F32))
            self.banks = [(self.ps[:, i * 512:(i + 1) * 512], Res("bank%d" % i, excl=True)) for i in range(8)]
            self.rot = Rot(self.banks[0:4])

            S.dma("sp", self.cst_f[:], I["cst"], writes=[self.r_cst])
            with ExitStack() as ist:
                sel_f = self.sb(ist, "sel_f", [16, 2048], F32)
                r_self = S.pres("sel_f")
                S.dma("sp", sel_f[:], I["selc"], writes=[r_self])
                S.op("dve", lambda e: e.tensor_copy(self.sel_b[:], sel_f[:]), reads=[r_self], writes=[self.r_cst])
            S.phase_end()
            S.dma("sp", self.gcols[:], I["gcols"], writes=[self.r_cst])
            S.dma("sp", self.wconv_c[:], I["wconv_c"], writes=[self.r_cst])
            S.dma("sp", self.bconv_c[:], I["bconv_c"], writes=[self.r_cst])
            S.dma("sp", self.hb[:], I["hb"], writes=[self.r_cst])
            S.op("dve", lambda e: e.tensor_copy(self.cst_b[:], self.cst_f[:]), reads=[self.r_cst], writes=[self.r_cst])
            S.op("dve", lambda e: e.memset(self.cc[:, 0:1], EPS), writes=[self.r_cst])
            S.op("dve", lambda e: e.memset(self.cc[:, 1:2], 1.0), writes=[self.r_cst])
            S.op("act", lambda e: e.activation(self.A_b[:], self.hb[:, 32:64], AF.Exp), reads=[self.r_cst], writes=[self.r_cst])
            S.op("dve", lambda e: e.tensor_scalar(self.A_b[:], self.A_b[:], -1.0, None, ALU.mult), reads=[self.r_cst], writes=[self.r_cst])

            plan = []
            for s in range(self.n_prompt_seq):
                plan += MKV_WORDER
                for t in range(self.prompt_tiles):
                    plan += TILE_WORDER
            if self.do_sample:
                plan += TILE_WORDER
            self.wplan = plan
            self.wpos = 0
            self.wloaded = {}
            self.conv_done = set()
            self.conv_next = 0
            self._convert_upto(0)

            try:
                self.body()
            except StopBuild:
                pass
            S.finish()
        return nc

    def ckpt(self, name):
        import os
        if os.environ.get("KSTOP") == name:
            raise StopBuild()

    def body(self):
        if True:
            I, O, S = self.I, self.O, self.S
            self.ckpt("stage0")
            tiles = []
            for s in range(self.n_prompt_seq):
                for t in range(self.prompt_tiles):
                    tiles.append(dict(kind="p", s=s, t=t, TT=512, BS=128, blk0=4 * t, x_src=I["xp"][s, 512 * t:512 * (t + 1), :],
                                      y_out=O["yp"][s, 512 * t:512 * (t + 1), :],
                                      fk_out=O["fkp"][s, 512 * t:512 * (t + 1), :],
                                      fv_out=O["fvp"][s, 512 * t:512 * (t + 1), :],
                                      fl_out=O["flp"][s, 512 * t:512 * (t + 1), :]))
            if self.do_sample:
                tiles.append(dict(kind="s", s=0, t=0, TT=64, BS=64, blk0=8, x_src=I["xs"], y_out=O["ys"], fk_out=O["fks"],
                                  fv_out=O["fvs"], fl_out=O["fls"]))
            for i, tl in enumerate(tiles):
                if tl["kind"] == "p" and tl["t"] == 0:
                    self.seq_init_prompt()
                    self.mem_prologue_prompt(tl["s"])
                    self.ckpt("memp")
                if tl["kind"] == "s":
                    self.sample_prologue()
                    if S.halted:
                        S.halted = False
                        raise StopBuild()
                    self.ckpt("sprol")
                if i == 0:
                    self.tile_t1(tl["TT"], tl["BS"], tl["x_src"])
                nxt = tiles[i + 1] if i + 1 < len(tiles) else None
                self.tile(tl["TT"], tl["BS"], tl["blk0"], tl["x_src"], tl["y_out"], tl["fk_out"], tl["fv_out"], tl["fl_out"], nxt)
                if tl["kind"] == "p" and tl["t"] == self.prompt_tiles - 1:
                    self.seq_epilogue(O["ssdp"][tl["s"]], O["convp"][tl["s"]])
                if tl["kind"] == "s":
                    self.seq_epilogue(O["ssds"], O["convs"])

    def wget(self, name):
        assert self.wplan[self.wpos] == name, (self.wplan[self.wpos], name)
        if self.wpos not in self.wloaded:
            self._wload(self.wpos)
        cur = self.wloaded.pop(self.wpos)
        self.wpos += 1
        if self.wpos < len(self.wplan) and self.wpos not in self.wloaded:
            self._wload(self.wpos)
            if self.wplan[self.wpos] == "dtff" and self.wpos + 1 < len(self.wplan) and (self.wpos + 1) not in self.wloaded:
                self._wload(self.wpos + 1)
        return cur

    def _convert_upto(self, pos):
        LOOK = int(_os.environ.get("KLOOK", "8"))
        while self.conv_next < len(self.wplan) and self.conv_next <= pos + LOOK:
            n = self.wplan[self.conv_next]
            self.conv_next += 1
            if n in self.conv_done:
                continue
            self.conv_done.add(n)
            src, parts, tot = self.chunks[n]
            for (c0, ncol, d0) in parts:
                self.S.dma("pool", self.wscr[n][:, :, d0:d0 + ncol],
                           self.I[src][:, c0:c0 + ncol].rearrange("(fc p) n -> p fc n", p=128),
                           writes=[self.r_wscr[n]])

    def _wload(self, pos):
        self._convert_upto(pos)
        n = self.wplan[pos]
        buf, r = self.wbuf_dtff if n == "dtff" else self.wbufs.next()
        ncol = self.chunks[n][2]
        self.S.dma("sp", buf[:, :, 0:ncol], self.wscr[n], reads=[self.r_wscr[n]], writes=[r])
        self.wloaded[pos] = (buf, r)

    def seq_init_prompt(self):
        S = self.S
        S.op("pool", lambda e: e.memset(self.stateT[:], 0.0), writes=[self.r_state])
        S.op("pool", lambda e: e.memset(self.hinT[:], 0.0), writes=[self.r_hinT])
        S.op("pool", lambda e: e.memset(self.convhist[:], 0.0), writes=[self.r_convhist])
        S.op("pool", lambda e: e.memset(self.carry[:], 0.0), writes=[self.r_carry])

    def norm_transpose(self, ph, src, BS, dstT, c0, gcol, r_dst, q="sp"):
        xs, r_xs = self.norm_load(ph, src, BS, q)
        self.norm_tr(xs, r_xs, BS, dstT, c0, gcol, r_dst)

    def norm_load(self, ph, src, BS, q="sp"):
        S = self.S
        xs, r_xs = ph["xs"].next()
        S.dma(q, xs[:BS, :], src, writes=[r_xs])
        junk, r_junk = ph["junk"]
        sm, r_sm = ph["small"].next()
        S.op("act", lambda e: e.activation(junk[:BS, :], xs[:BS, :], AF.Square, accum_out=sm[:BS, 0:1]),
             reads=[r_xs], writes=[r_junk, r_sm])
        S.op("act", lambda e: e.activation(sm[:BS, 1:2], sm[:BS, 0:1], AF.Sqrt, bias=self.cc[:BS, 0:1], scale=1.0 / D),
             reads=[r_sm, self.r_cst], writes=[r_sm])
        S.op("dve", lambda e: e.reciprocal(sm[:BS, 2:3], sm[:BS, 1:2]), reads=[r_sm], writes=[r_sm])
        S.op("dve", lambda e: e.tensor_scalar(xs[:BS, :], xs[:BS, :], sm[:BS, 2:3], None, ALU.mult),
             reads=[r_xs, r_sm], writes=[r_xs])
        return xs, r_xs

    def norm_tr(self, xs, r_xs, BS, dstT, c0, gcol, r_dst):
        S = self.S
        for q in range(4):
            bank, r_b = self.rot.next()
            for k in range(4):
                fc = q * 4 + k
                S.op("pe", lambda e: e.transpose(bank[:, k * BS:(k + 1) * BS], xs[:BS, fc * 128:(fc + 1) * 128],
                                                 self.ident_f[:BS, :BS]),
                     reads=[r_xs, self.r_cst], writes=[r_b])
            S.op("dve", lambda e: e.tensor_tensor(
                dstT[:, q * 4:(q + 1) * 4, c0:c0 + BS],
                bank[:, 0:4 * BS].rearrange("p (k b) -> p k b", k=4),
                gcol[:, q * 4:(q + 1) * 4].unsqueeze(2).to_broadcast([128, 4, BS]), ALU.mult),
                reads=[r_b, self.r_cst], writes=[r_dst])

    def proj_tok(self, lhsT_tile, c0, BS, w, r_w, ncol, reads):
        bank, r_b = self.rot.next()
        for fc in range(16):
            self.S.mm(bank[:BS, 0:ncol], lhsT_tile[:, fc, c0:c0 + BS], w[:, fc, 0:ncol], start=(fc == 0), stop=(fc == 15),
                      reads=reads + [r_w], writes=[r_b])
        return bank, r_b

    def proj_feat(self, w, r_w, wc0, rhs_tile, TT, reads, bank=None):
        if bank is None:
            bank, r_b = self.rot.next()
        else:
            bank, r_b = bank
        for fc in range(16):
            self.S.mm(bank[:, 0:TT], w[:, fc, wc0:wc0 + 128], rhs_tile[:, fc, 0:TT], start=(fc == 0), stop=(fc == 15),
                      reads=reads + [r_w], writes=[r_b])
        return bank, r_b

    def mem_prologue_prompt(self, s):
        S, nc = self.S, self.nc
        with ExitStack() as pst:
            sb = lambda name, shape, dt: self.sb(pst, name, shape, dt)
            ph = {
                "xs": Rot([(sb("mp_xs%d" % i, [128, 2048], F32), S.pres("mp_xs")) for i in range(2)]),
                "junk": (sb("mp_junk", [128, 2048], BF16), S.pres("mp_junk")),
                "small": Rot([(sb("mp_sm%d" % i, [128, 4], F32), S.pres("mp_sm")) for i in range(2)]),
            }
            mT = sb("mp_mT", [128, 16, 256], BF16)
            r_mT = S.pres("mp_mT")
            memKT = sb("mp_memKT", [128, 16, 256], BF16)
            r_memKT = S.pres("mp_memKT")
            memV = sb("mp_memV", [128, 2, 2048], BF16)
            r_memV = S.pres("mp_memV")
            stg = Rot([(sb("mp_stg%d" % i, [128, 512], F32), S.pres("mp_stg")) for i in range(3)])
            for mb in range(2):
                self.norm_transpose(ph, self.I["memp"][s, mb * 128:(mb + 1) * 128, :], 128, mT, mb * 128,
                                    self.gcols[:, 32:48], r_mT)
            for j in range(8):
                w, r_w = self.wget("mkv%d" % j)
                if j < 4:
                    for cc in range(4):
                        bank, r_b = self.proj_feat(w, r_w, cc * 128, mT, 256, [r_mT])
                        S.op("act", lambda e: e.copy(memKT[:, j * 4 + cc, :], bank[:, 0:256]), reads=[r_b], writes=[r_memKT])
                for mb in range(2):
                    bank, r_b = self.proj_tok(mT, mb * 128, 128, w, r_w, 512, [r_mT])
                    sg, r_sg = stg.next()
                    S.op("dve", lambda e: e.tensor_copy(sg[:], bank[:, :]), reads=[r_b], writes=[r_sg])
                    if j < 4:
                        S.dma("act", self.O["mkp"][s, mb * 128:(mb + 1) * 128, j * 512:(j + 1) * 512], sg[:], reads=[r_sg])
                    else:
                        jj = j - 4
                        S.dma("act", self.O["mvp"][s, mb * 128:(mb + 1) * 128, jj * 512:(jj + 1) * 512], sg[:], reads=[r_sg])
                        S.op("act", lambda e: e.copy(memV[:, mb, jj * 512:(jj + 1) * 512], bank[:, :]), reads=[r_b],
                             writes=[r_memV])
            S.dma("act", self.memKT_d, memKT[:], reads=[r_memKT], writes=[self.r_memKT_d])
            S.dma("act", self.memV_d, memV[:], reads=[r_memV], writes=[self.r_memV_d])
        S.phase_end()

    def sample_prologue(self):
        S, I = self.S, self.I
        with ExitStack() as pst:
            sb = lambda name, shape, dt: self.sb(pst, name, shape, dt)
            ld = Rot([(sb("sp_ld%d" % i, [128, 2048], F32), S.pres("sp_ld")) for i in range(2)])
            stg = Rot([(sb("sp_stg%d" % i, [128, 16, 128], BF16), S.pres("sp_stg")) for i in range(2)])
            memKT = sb("sp_memKT", [128, 16, 256], BF16)
            r_memKT = S.pres("sp_memKT")
            lfh = sb("sp_lfh", [128, 8, 16], F32)
            r_lfh = S.pres("sp_lfh")
            ctmp = sb("sp_ctmp", [128, 8, 16], F32)
            r_ctmp = S.pres("sp_ctmp")
            for q in range(4):
                x, r_x = ld.next()
                S.dma("sp", x[:, 0:512].rearrange("p (a n) -> p a n", a=4),
                      I["sst"][q * 512:(q + 1) * 512, :].rearrange("(a p) n -> p a n", p=128), writes=[r_x])
                bank, r_b = self.rot.next()
                for a in range(4):
                    S.op("pe", lambda e: e.transpose(bank[:, a * 128:(a + 1) * 128], x[:, a * 128:(a + 1) * 128], self.ident_f),
                         reads=[r_x, self.r_cst], writes=[r_b])
                S.op("dve", lambda e: e.tensor_copy(self.stateT[:, q * 512:(q + 1) * 512], bank[:, :]), reads=[r_b],
                     writes=[self.r_state])
                S.op("act", lambda e: e.copy(self.hinT[:, q * 512:(q + 1) * 512], bank[:, :]), reads=[r_b], writes=[self.r_hinT])
            if _os.environ.get("KSTOP") == "sp_a": S.halted = True
            for r in range(3):
                S.dma("sp", self.convhist[:, :, r], I["sconv"][r].rearrange("(c p) -> p c", p=128), writes=[self.r_convhist],
                      allow_slow_non_contiguous=True)
            if _os.environ.get("KSTOP") == "sp_b": S.halted = True
            for mb in range(2):
                x, r_x = ld.next()
                S.dma("sp", x[:], I["cmk"][mb * 128:(mb + 1) * 128, :], writes=[r_x])
                for q in range(4):
                    bank, r_b = self.rot.next()
                    for k in range(4):
                        c = q * 4 + k
                        S.op("pe", lambda e: e.transpose(bank[:, k * 128:(k + 1) * 128], x[:, c * 128:(c + 1) * 128], self.ident_f),
                             reads=[r_x, self.r_cst], writes=[r_b])
                    S.op("act", lambda e: e.copy(memKT[:, q * 4:(q + 1) * 4, mb * 128:(mb + 1) * 128],
                                                 bank[:, :].rearrange("p (k b) -> p k b", k=4)), reads=[r_b], writes=[r_memKT])
            S.dma("act", self.memKT_d, memKT[:], reads=[r_memKT], writes=[self.r_memKT_d])
            vstg = Rot([(sb("sp_vstg%d" % i, [128, 2048], BF16), S.pres("sp_vstg")) for i in range(2)])
            for mb in range(2):
                x, r_x = ld.next()
                S.dma("sp", x[:], I["cmv"][mb * 128:(mb + 1) * 128, :], writes=[r_x])
                vs, r_vs = vstg.next()
                S.op("pool", lambda e: e.tensor_copy(vs[:], x[:]), reads=[r_x], writes=[r_vs])
                S.dma("act", self.memV_d[:, mb, :], vs[:], reads=[r_vs], writes=[self.r_memV_d])
            if _os.environ.get("KSTOP") == "sp_c": S.halted = True
            for blk in range(8):
                x, r_x = ld.next()
                S.dma("sp", x[:], I["ck"][blk * 128:(blk + 1) * 128, :], writes=[r_x])
                sg, r_sg = stg.next()
                for q in range(4):
                    bank, r_b = self.rot.next()
                    for k in range(4):
                        h = q * 4 + k
                        S.op("pe", lambda e: e.transpose(bank[:, k * 128:(k + 1) * 128], x[:, h * 128:(h + 1) * 128], self.ident_f),
                             reads=[r_x, self.r_cst], writes=[r_b])
                    S.op("act" if q % 2 == 0 else "dve",
                         (lambda e: e.copy(sg[:, q * 4:(q + 1) * 4, :], bank[:, :].rearrange("p (k b) -> p k b", k=4))) if q % 2 == 0 else
                         (lambda e: e.tensor_copy(sg[:, q * 4:(q + 1) * 4, :], bank[:, :].rearrange("p (k b) -> p k b", k=4))),
                         reads=[r_b], writes=[r_sg])
                S.dma("act", self.kT_hist[:, :, blk * 128:(blk + 1) * 128].rearrange("h d t -> d h t"), sg[:],
                      reads=[r_sg], writes=[self.r_khist])
                x, r_x = ld.next()
                S.dma("sp", x[:], I["cv"][blk * 128:(blk + 1) * 128, :], writes=[r_x])
                vs, r_vs = vstg.next()
                S.op("pool", lambda e: e.tensor_copy(vs[:], x[:]), reads=[r_x], writes=[r_vs])
                S.dma("act", self.v_hist[:, :, blk, :].rearrange("h p d -> p h d"), vs[:].rearrange("p (h d) -> p h d", h=16),
                      reads=[r_vs], writes=[self.r_vhist])
            if _os.environ.get("KSTOP") == "sp_d": S.halted = True
            S.dma("sp", lfh[:], I["clf"].rearrange("(b p) h -> p b h", p=128), writes=[r_lfh])
            S.op("pool", lambda e: e.memset(self.carry[:], 0.0), writes=[self.r_carry])
            self.cumsum_blocks(lfh, r_lfh, 128, 8, 0, ctmp, r_ctmp)
        S.phase_end()

    def cumsum_blocks(self, lf, r_lf, BS, NB, blk0, ctmp, r_ctmp):
        S = self.S
        bank, r_b = self.rot.next()
        n = NB * 16
        S.mm(bank[:BS, 0:n], self.triU_f[:BS, :BS], lf[:BS, 0:NB, :].rearrange("p b h -> p (b h)"), start=True, stop=True,
             reads=[r_lf, self.r_cst], writes=[r_b])
        S.mm(bank[:, 256:256 + n], self.ones_f[:BS, :], lf[:BS, 0:NB, :].rearrange("p b h -> p (b h)"), start=True, stop=True,
             reads=[r_lf, self.r_cst], writes=[r_b])
        S.op("dve", lambda e: e.tensor_copy(ctmp[:, 0:NB, :].rearrange("p b h -> p (b h)"), bank[:, 256:256 + n]),
             reads=[r_b], writes=[r_ctmp])
        for b in range(NB):
            S.op("dve", lambda e: e.scalar_tensor_tensor(self.negc[:BS, blk0 + b, :], bank[:BS, b * 16:(b + 1) * 16], -1.0,
                                                         self.carry[:BS, :], ALU.mult, ALU.subtract),
                 reads=[r_b, self.r_carry], writes=[self.r_negc])
            S.op("dve", lambda e: e.tensor_tensor(self.carry[:], self.carry[:], ctmp[:, b, :], ALU.add),
                 reads=[self.r_carry, r_ctmp], writes=[self.r_carry])

    def seq_epilogue(self, ssd_out, conv_out):
        S = self.S
        with ExitStack() as pst:
            sb = lambda name, shape, dt: self.sb(pst, name, shape, dt)
            stg = Rot([(sb("ep_stg%d" % i, [128, 4, 128], F32), S.pres("ep_stg")) for i in range(2)])
            for q in range(4):
                bank, r_b = self.rot.next()
                for a in range(4):
                    c = q * 4 + a
                    S.op("pe", lambda e: e.transpose(bank[:, a * 128:(a + 1) * 128], self.stateT[:, c * 128:(c + 1) * 128], self.ident_f),
                         reads=[self.r_state, self.r_cst], writes=[r_b])
                sg, r_sg = stg.next()
                S.op("dve", lambda e: e.tensor_copy(sg[:], bank[:, :].rearrange("p (a n) -> p a n", a=4)), reads=[r_b], writes=[r_sg])
                S.dma("act", ssd_out[q * 512:(q + 1) * 512, :].rearrange("(a p) n -> p a n", p=128), sg[:], reads=[r_sg])
            for r in range(3):
                S.dma("act", conv_out[r].rearrange("(c p) -> p c", p=128), self.convhist[:, :, r], reads=[self.r_convhist],
                      allow_slow_non_contiguous=True)
        S.phase_end()

    def branch_project(self, ph, yT, r_yT, wname, gname, first, TT):
        S = self.S
        for j in range(4):
            w, r_w = self.wget("%s%d" % (wname, j))
            pbanks = []
            for cc in range(4):
                pbanks.append(self.proj_feat(w, r_w, cc * 128, yT, TT, [r_yT], bank=self.banks[4 + cc]))
            g, r_g = self.wget("%s%d" % (gname, j))
            for cc in range(4):
                oc = j * 4 + cc
                gb, r_gb = self.proj_feat(g, r_g, cc * 128, self.hT, TT, [self.r_hT])
                sg, r_sg = ph["sig"].next()
                S.op("act", lambda e: e.activation(sg[:, 0:TT], gb[:, 0:TT], AF.Sigmoid), reads=[r_gb], writes=[r_sg])
                pb, r_pb = pbanks[cc]
                if first:
                    S.op("dve", lambda e: e.tensor_tensor(self.mergedT[:, oc, 0:TT], pb[:, 0:TT], sg[:, 0:TT], ALU.mult),
                         reads=[r_pb, r_sg], writes=[self.r_merged])
                else:
                    S.op("dve", lambda e: e.tensor_tensor(sg[:, 0:TT], pb[:, 0:TT], sg[:, 0:TT], ALU.mult),
                         reads=[r_pb, r_sg], writes=[r_sg])
                    S.op("dve", lambda e: e.tensor_tensor(self.mergedT[:, oc, 0:TT], self.mergedT[:, oc, 0:TT], sg[:, 0:TT], ALU.add),
                         reads=[r_sg, self.r_merged], writes=[self.r_merged])

    def tile_t1(self, TT, BS, x_src):
        S = self.S
        NB = TT // BS
        with ExitStack() as pst:
            sb = lambda name, shape, dt: self.sb(pst, name, shape, dt)
            ph = {
                "xs": Rot([(sb("t1_xs%d" % i, [128, 2048], F32), S.pres("t1_xs")) for i in range(2)]),
                "junk": (sb("t1_junk", [128, 2048], BF16), S.pres("t1_junk")),
                "small": Rot([(sb("t1_sm%d" % i, [128, 4], F32), S.pres("t1_sm")) for i in range(2)]),
            }
            for tb in range(NB):
                self.norm_transpose(ph, x_src[tb * BS:(tb + 1) * BS, :], BS, self.hT, tb * BS, self.gcols[:, 0:16], self.r_hT, q="act")
        S.phase_end()

    def tile(self, TT, BS, blk0, x_src, y_out, fk_out, fv_out, fl_out, nxt):
        NB = TT // BS
        self.ckpt("t1")
        self.tile_ssd(TT, BS, NB, blk0, fl_out)
        self.ckpt("ssd")
        self.tile_fox(TT, BS, NB, blk0, fk_out, fv_out)
        self.ckpt("fox")
        self.tile_mem(TT, BS, NB, nxt)
        self.ckpt("mem")
        self.tile_final(TT, BS, NB, x_src, y_out)
        self.ckpt("final")

    def tile_ssd(self, TT, BS, NB, blk0, fl_out):
        S = self.S
        hT = self.hT
        with ExitStack() as pst:
            sb = lambda name, shape, dt: self.sb(pst, name, shape, dt)
            ph = {"sig": Rot([(sb("s_sig%d" % i, [128, 512], BF16), S.pres("s_sig")) for i in range(2)])}
            yT = sb("s_yT", [128, 16, 512], BF16); r_yT = S.pres("s_yT")
            zs = sb("s_zs", [128, 4, 2048], BF16); r_zs = S.pres("s_zs")
            x_tok = sb("s_xtok", [128, 4, 2048], BF16); r_xtok = S.pres("s_xtok")
            BT = sb("s_BT", [128, 4, 512], BF16); r_BT = S.pres("s_BT")
            CT = sb("s_CT", [128, 4, 512], BF16); r_CT = S.pres("s_CT")
            B_tok = sb("s_Btok", [128, 4, 512], BF16); r_Btok = S.pres("s_Btok")
            sm = sb("s_sm", [128, 9, 128], F32); r_sm = S.pres("s_sm")
            dtr, dt, dtA, acum, ea, tot, dtot, wend, ffb = [sm[:, i, :] for i in range(9)]
            e1, e2, lfn = dtr, ffb, ffb
            ctmp = sb("s_ctmp", [128, 4, 16], F32); r_ctmp = S.pres("s_ctmp")
            xp = Rot([(sb("s_xp%d" % i, [128, 520], F32), S.pres("s_xp")) for i in range(2)])
            cacc = Rot([(sb("s_cacc%d" % i, [128, 512], F32), S.pres("s_cacc")) for i in range(2)])
            xc = Rot([(sb("s_xc%d" % i, [128, 512], BF16), S.pres("s_xc")) for i in range(3)])
            xdt = sb("s_xdt", [128, 2048], BF16); r_xdt = S.pres("s_xdt")
            xw = sb("s_xw", [128, 2048], BF16); r_xw = S.pres("s_xw")
            yblk = [(sb("s_yblk%d" % i, [128, 2048], BF16), S.pres("s_yblk")) for i in range(2)]
            yn = sb("s_yn", [128, 2048], BF16); r_yn = S.pres("s_yn")
            rhsD1 = sb("s_rhsD", [128, 8, 128], F32)
            r_rhsD1 = S.pres("s_rhsD")
            scanb = Rot([(rhsD1, r_rhsD1, sb("s_E%d" % i, [128, 8, 128], BF16), S.pres("s_E"),
                          sb("s_MT%d" % i, [128, 8, 128], BF16), S.pres("s_MT")) for i in range(2)])
            CBm = sb("s_CBm", [128, 4, 128], BF16); r_CBm = S.pres("s_CBm")
            tmpg = Rot([(sb("s_tmpg%d" % i, [128, 512], F32), S.pres("s_tmpg")) for i in range(4)])
            st4 = sb("s_st4", [128, 4], F32); r_st4 = S.pres("s_st4")

            n32 = NB * 32
            w, r_w = self.wget("dtff")
            for tb in range(NB):
                bank, r_b = self.proj_tok(hT, tb * BS, BS, w, r_w, 48, [self.r_hT])
                S.op("dve", lambda e: e.tensor_tensor(dtr[:BS, tb * 32:(tb + 1) * 32], bank[:BS, 0:32], self.hb[:BS, 0:32], ALU.add),
                     reads=[r_b, self.r_cst], writes=[r_sm])
                S.op("dve", lambda e: e.tensor_tensor(ffb[:BS, tb * 16:(tb + 1) * 16], bank[:BS, 32:48], self.hb[:BS, 96:112], ALU.add),
                     reads=[r_b, self.r_cst], writes=[r_sm])
            n16 = NB * 16
            S.op("act", lambda e: e.activation(e1[:BS, 0:n32], dtr[:BS, 0:n32], AF.Exp), reads=[r_sm], writes=[r_sm])
            S.op("act", lambda e: e.activation(e2[:BS, 0:n16], ffb[:BS, 0:n16], AF.Exp, scale=-1.0), reads=[r_sm], writes=[r_sm])
            S.op("act", lambda e: e.activation(dt[:BS, 0:n32], e1[:BS, 0:n32], AF.Ln, bias=self.cc[:BS, 1:2]), reads=[r_sm, self.r_cst], writes=[r_sm])
            S.op("act", lambda e: e.activation(lfn[:BS, 0:n16], e2[:BS, 0:n16], AF.Ln, bias=self.cc[:BS, 1:2]), reads=[r_sm, self.r_cst], writes=[r_sm])
            S.op("dve", lambda e: e.tensor_scalar(self.lf_tok[:BS, 0:NB, :].rearrange("p b h -> p (b h)"), lfn[:BS, 0:n16], -1.0, None, ALU.mult),
                 reads=[r_sm], writes=[self.r_lf])
            S.dma("act", fl_out.rearrange("(b p) h -> p b h", p=BS), self.lf_tok[:BS, 0:NB, :], reads=[self.r_lf])
            for tb in range(NB):
                S.op("dve", lambda e: e.tensor_tensor(dtA[:BS, tb * 32:(tb + 1) * 32], dt[:BS, tb * 32:(tb + 1) * 32], self.A_b[:BS, :], ALU.mult),
                     reads=[r_sm, self.r_cst], writes=[r_sm])
            for j in range(4):
                w, r_w = self.wget("z%d" % j)
                for tb in range(NB):
                    bank, r_b = self.proj_tok(hT, tb * BS, BS, w, r_w, 512, [self.r_hT])
                    S.op("act", lambda e: e.activation(zs[:BS, tb, j * 512:(j + 1) * 512], bank[:BS, :], AF.Silu), reads=[r_b], writes=[r_zs])
            self.cumsum_blocks(self.lf_tok, self.r_lf, BS, NB, blk0, ctmp, r_ctmp)
            bank, r_b = self.rot.next()
            S.mm(bank[:BS, 0:n32], self.triU_f[:BS, :BS], dtA[:BS, 0:n32], start=True, stop=True, reads=[r_sm, self.r_cst], writes=[r_b])
            S.mm(bank[:, 256:256 + n32], self.ones_f[:BS, :], dtA[:BS, 0:n32], start=True, stop=True, reads=[r_sm, self.r_cst], writes=[r_b])
            S.op("dve", lambda e: e.tensor_copy(acum[:BS, 0:n32], bank[:BS, 0:n32]), reads=[r_b], writes=[r_sm])
            S.op("dve", lambda e: e.tensor_copy(tot[:, 0:n32], bank[:, 256:256 + n32]), reads=[r_b], writes=[r_sm])
            S.op("act", lambda e: e.activation(ea[:BS, 0:n32], acum[:BS, 0:n32], AF.Exp), reads=[r_sm], writes=[r_sm])
            S.op("act", lambda e: e.activation(dtot[:, 0:n32], tot[:, 0:n32], AF.Exp), reads=[r_sm], writes=[r_sm])
            S.op("dve", lambda e: e.tensor_tensor(wend[:BS, 0:n32], tot[:BS, 0:n32], acum[:BS, 0:n32], ALU.subtract), reads=[r_sm], writes=[r_sm])
            S.op("act", lambda e: e.activation(wend[:BS, 0:n32], wend[:BS, 0:n32], AF.Exp), reads=[r_sm], writes=[r_sm])
            S.op("dve", lambda e: e.tensor_tensor(wend[:BS, 0:n32], wend[:BS, 0:n32], dt[:BS, 0:n32], ALU.mult), reads=[r_sm], writes=[r_sm])
            DELAY = 2
            pendB = []

            def stageB(ch, src, r_dst):
                bank, r_b = self.rot.next()
                bb = bank.bitcast(BF16)
                for tb in range(NB):
                    S.op("pe", lambda e: e.transpose(bb[:BS, tb * 128:(tb + 1) * 128], src[:, tb * BS:(tb + 1) * BS], self.ident_b),
                         reads=[r_dst, self.r_cst], writes=[r_b])
                if ch < 16:
                    S.op("dve", lambda e: e.tensor_copy(x_tok[:BS, 0:NB, ch * 128:(ch + 1) * 128], bb[:BS, 0:NB * 128].rearrange("p (b c) -> p b c", b=NB)),
                         reads=[r_b], writes=[r_xtok])
                else:
                    g = ch - 16
                    S.op("dve", lambda e: e.tensor_copy(B_tok[:BS, 0:NB, g * 128:(g + 1) * 128], bb[:BS, 0:NB * 128].rearrange("p (b c) -> p b c", b=NB)),
                         reads=[r_b], writes=[r_Btok])

            for j in range(6):
                w, r_w = self.wget("xbc%d" % j)
                for cc in range(4):
                    ch = j * 4 + cc
                    bank, r_b = self.proj_feat(w, r_w, cc * 128, hT, TT, [self.r_hT])
                    x_, r_x = xp.next()
                    S.op("act", lambda e: e.copy(x_[:, 3:3 + TT], bank[:, 0:TT]), reads=[r_b], writes=[r_x])
                    S.op("pool", lambda e: e.tensor_copy(x_[:, 0:3], self.convhist[:, ch, :]), reads=[self.r_convhist], writes=[r_x])
                    a_, r_a = cacc.next()
                    S.op("dve", lambda e: e.tensor_scalar(a_[:, 0:TT], x_[:, 0:TT], self.wconv_c[:, ch * 4:ch * 4 + 1], self.bconv_c[:, ch:ch + 1], ALU.mult, ALU.add),
                         reads=[r_x, self.r_cst], writes=[r_a])
                    for i in range(1, 4):
                        S.op("dve", lambda e: e.scalar_tensor_tensor(a_[:, 0:TT], x_[:, i:i + TT], self.wconv_c[:, ch * 4 + i:ch * 4 + i + 1], a_[:, 0:TT], ALU.mult, ALU.add),
                             reads=[r_x, r_a, self.r_cst], writes=[r_a])
                    S.op("pool", lambda e: e.tensor_copy(self.convhist[:, ch, :], x_[:, TT:TT + 3]), reads=[r_x], writes=[self.r_convhist])
                    if ch < 16:
                        c_, r_c = xc.next()
                        dst, r_dst, src = c_[:, 0:TT], r_c, c_
                    elif ch < 20:
                        dst, r_dst, src = BT[:, ch - 16, 0:TT], r_BT, BT[:, ch - 16, :]
                    else:
                        dst, r_dst, src = CT[:, ch - 20, 0:TT], r_CT, CT[:, ch - 20, :]
                    S.op("act", lambda e: e.activation(dst, a_[:, 0:TT], AF.Silu), reads=[r_a], writes=[r_dst])
                    if ch < 20:
                        pendB.append((ch, src, r_dst))
                    while pendB and pendB[0][0] <= ch - DELAY:
                        stageB(*pendB.pop(0))
            while pendB:
                stageB(*pendB.pop(0))
            items = [(tb, g) for tb in range(NB) for g in range(4)]
            st1 = {}

            def block_prep(tb):
                t0, t1 = tb * BS, (tb + 1) * BS
                h0 = tb * 32
                S.op("dve", lambda e: e.tensor_tensor(xdt[:BS, :].rearrange("p (h d) -> p h d", h=32), x_tok[:BS, tb, :].rearrange("p (h d) -> p h d", h=32),
                                                      dt[:BS, h0:h0 + 32].unsqueeze(2).to_broadcast([BS, 32, 64]), ALU.mult),
                     reads=[r_xtok, r_sm], writes=[r_xdt])
                S.op("pool", lambda e: e.tensor_tensor(xw[:BS, :].rearrange("p (h d) -> p h d", h=32), x_tok[:BS, tb, :].rearrange("p (h d) -> p h d", h=32),
                                                       wend[:BS, h0:h0 + 32].unsqueeze(2).to_broadcast([BS, 32, 64]), ALU.mult),
                     reads=[r_xtok, r_sm], writes=[r_xw])
                bank, r_b = self.rot.next()
                for g in range(4):
                    S.mm(bank[:BS, g * 128:g * 128 + BS], BT[:, g, t0:t1], CT[:, g, t0:t1], start=True, stop=True, reads=[r_BT, r_CT], writes=[r_b])
                S.op("dve", lambda e: e.tensor_tensor(CBm[:BS, :, 0:BS], bank[:BS, :].rearrange("p (g l) -> p g l", g=4)[:, :, 0:BS],
                                                      self.triU_f[:BS, 0:BS].unsqueeze(1).to_broadcast([BS, 4, BS]), ALU.mult),
                     reads=[r_b, self.r_cst], writes=[r_CBm])

            def stage1(tb, g):
                hh0 = tb * 32 + g * 8
                rhsD, r_rhsD, Et, r_E, MT, r_MT = scanb.next()
                S.op("pool", lambda e: e.tensor_tensor(rhsD[:BS, :, 0:BS], dtA[:BS, hh0:hh0 + 8].unsqueeze(2).to_broadcast([BS, 8, BS]),
                                                       self.triU_f[:BS, 0:BS].unsqueeze(1).to_broadcast([BS, 8, BS]), ALU.mult),
                     reads=[r_sm, self.r_cst], writes=[r_rhsD])
                par = (tb * 4 + g) % 2
                for half in range(2):
                    bD, r_bD = self.banks[4 + 2 * par + half]
                    S.mm(bD[:BS, 0:4 * BS], self.triLs_f[:BS, :BS], rhsD[:BS, half * 4:(half + 1) * 4, 0:BS], start=True, stop=True,
                         reads=[r_rhsD, self.r_cst], writes=[r_bD])
                    S.op("act", lambda e: e.activation(Et[:BS, half * 4:(half + 1) * 4, 0:BS], bD[:BS, 0:4 * BS].rearrange("p (j l) -> p j l", j=4), AF.Exp),
                         reads=[r_bD], writes=[r_E])
                tg2, r_tg2 = tmpg.next()
                S.op("pool", lambda e: e.tensor_tensor(tg2[:BS, :].rearrange("p (j d) -> p j d", j=8),
                                                       x_tok[:BS, tb, g * 512:(g + 1) * 512].rearrange("p (j d) -> p j d", j=8),
                                                       self.hb[:BS, 64 + g * 8:64 + g * 8 + 8].unsqueeze(2).to_broadcast([BS, 8, 64]), ALU.mult),
                     reads=[r_xtok, self.r_cst], writes=[r_tg2])
                st1[(tb, g)] = (Et, r_E, MT, r_MT, tg2, r_tg2)

            def stage2(tb, g):
                t0, t1 = tb * BS, (tb + 1) * BS
                hh0 = tb * 32 + g * 8
                Et, r_E, MT, r_MT, tg2, r_tg2 = st1.pop((tb, g))
                yb, r_yb = yblk[tb % 2]
                S.op("dve", lambda e: e.tensor_tensor(MT[:BS, :, 0:BS], Et[:BS, :, 0:BS], CBm[:BS, g, 0:BS].unsqueeze(1).to_broadcast([BS, 8, BS]), ALU.mult),
                     reads=[r_E, r_CBm], writes=[r_MT])
                byd, r_byd = self.rot.next()
                for j in range(8):
                    hh = g * 8 + j
                    S.mm(byd[:BS, j * 64:(j + 1) * 64], MT[:BS, j, 0:BS], xdt[:BS, hh * 64:(hh + 1) * 64], start=True, stop=True,
                         reads=[r_MT, r_xdt], writes=[r_byd])
                byo, r_byo = self.rot.next()
                S.mm(byo[:BS, :], CT[:, g, t0:t1], self.hinT[:, g * 512:(g + 1) * 512], start=True, stop=True,
                     reads=[r_CT, self.r_hinT], writes=[r_byo])
                tg, r_tg = tmpg.next()
                S.op("dve", lambda e: e.tensor_tensor(tg[:BS, :].rearrange("p (j d) -> p j d", j=8), byo[:BS, :].rearrange("p (j d) -> p j d", j=8),
                                                      ea[:BS, hh0:hh0 + 8].unsqueeze(2).to_broadcast([BS, 8, 64]), ALU.mult),
                     reads=[r_byo, r_sm], writes=[r_tg])
                S.op("dve", lambda e: e.tensor_tensor(tg[:BS, :], tg[:BS, :], byd[:BS, :], ALU.add), reads=[r_tg, r_byd], writes=[r_tg])
                S.op("pool", lambda e: e.tensor_tensor(tg[:BS, :], tg[:BS, :], tg2[:BS, :], ALU.add), reads=[r_tg, r_tg2], writes=[r_tg])
                S.op("dve", lambda e: e.tensor_tensor(yb[:BS, g * 512:(g + 1) * 512], tg[:BS, :], zs[:BS, tb, g * 512:(g + 1) * 512], ALU.mult),
                     reads=[r_tg, r_zs], writes=[r_yb])

            def state_update(tb):
                h0 = tb * 32
                for g in range(4):
                    hh0 = h0 + g * 8
                    bst, r_bst = self.rot.next()
                    S.mm(bst[:, :], B_tok[:BS, tb, g * 128:(g + 1) * 128], xw[:BS, g * 512:(g + 1) * 512], start=True, stop=True,
                         reads=[r_Btok, r_xw], writes=[r_bst])
                    S.op("dve", lambda e: e.tensor_tensor(self.stateT[:, g * 512:(g + 1) * 512].rearrange("p (j d) -> p j d", j=8),
                                                          self.stateT[:, g * 512:(g + 1) * 512].rearrange("p (j d) -> p j d", j=8),
                                                          dtot[:, hh0:hh0 + 8].unsqueeze(2).to_broadcast([128, 8, 64]), ALU.mult),
                         reads=[self.r_state, r_sm], writes=[self.r_state])
                    S.op("dve", lambda e: e.tensor_tensor(self.stateT[:, g * 512:(g + 1) * 512], self.stateT[:, g * 512:(g + 1) * 512], bst[:, :], ALU.add),
                         reads=[self.r_state, r_bst], writes=[self.r_state])
                S.op("act", lambda e: e.copy(self.hinT[:], self.stateT[:]), reads=[self.r_state], writes=[self.r_hinT])

            def epilogue(tb):
                t0, t1 = tb * BS, (tb + 1) * BS
                yb, r_yb = yblk[tb % 2]
                S.op("act", lambda e: e.activation(yn[:BS, :], yb[:BS, :], AF.Square, accum_out=st4[:BS, 0:1]), reads=[r_yb], writes=[r_yn, r_st4])
                S.op("act", lambda e: e.activation(st4[:BS, 1:2], st4[:BS, 0:1], AF.Sqrt, bias=self.cc[:BS, 0:1], scale=1.0 / D), reads=[r_st4, self.r_cst], writes=[r_st4])
                S.op("dve", lambda e: e.reciprocal(st4[:BS, 2:3], st4[:BS, 1:2]), reads=[r_st4], writes=[r_st4])
                S.op("act", lambda e: e.activation(yn[:BS, :], yb[:BS, :], AF.Identity, scale=st4[:BS, 2:3]), reads=[r_yb, r_st4], writes=[r_yn])
                for q in range(4):
                    bank, r_b = self.rot.next()
                    bb = bank.bitcast(BF16)
                    for k in range(4):
                        fc = q * 4 + k
                        S.op("pe", lambda e: e.transpose(bb[:, k * BS:(k + 1) * BS], yn[:BS, fc * 128:(fc + 1) * 128], self.ident_b[:BS, :BS]),
                             reads=[r_yn, self.r_cst], writes=[r_b])
                    S.op("dve", lambda e: e.tensor_tensor(yT[:, q * 4:(q + 1) * 4, t0:t1], bb[:, 0:4 * BS].rearrange("p (k b) -> p k b", k=4),
                                                          self.gcols[:, 16 + q * 4:16 + (q + 1) * 4].unsqueeze(2).to_broadcast([128, 4, BS]), ALU.mult),
                         reads=[r_b, self.r_cst], writes=[r_yT])

            block_prep(0)
            stage1(*items[0])
            pend_epi = []
            for i, (tb, g) in enumerate(items):
                if i + 1 < len(items):
                    stage1(*items[i + 1])
                stage2(tb, g)
                if g == 1 and pend_epi:
                    epilogue(pend_epi.pop(0))
                if g == 3:
                    state_update(tb)
                    pend_epi.append(tb)
                    if tb + 1 < NB:
                        block_prep(tb + 1)
            while pend_epi:
                epilogue(pend_epi.pop(0))
            self.branch_project(ph, yT, r_yT, "wos", "gs", True, TT)
        S.phase_end()

    def tile_fox(self, TT, BS, NB, blk0, fk_out, fv_out):
        S = self.S
        hT = self.hT
        scale = 128.0 ** -0.5
        nhist = blk0
        with ExitStack() as pst:
            sb = lambda name, shape, dt: self.sb(pst, name, shape, dt)
            ph = {"sig": Rot([(sb("f_sig%d" % i, [128, 512], F32), S.pres("f_sig")) for i in range(2)])}
            yT = sb("f_yT", [128, 16, 512], BF16); r_yT = S.pres("f_yT")
            V_cur = sb("f_Vcur", [128, 4, 2048], BF16); r_Vcur = S.pres("f_Vcur")
            qT = sb("f_qT", [128, 4, 512], BF16); r_qT = S.pres("f_qT")
            kT = sb("f_kT", [128, 4, 512], BF16); r_kT = S.pres("f_kT")
            gT = sb("f_gT", [128, 4, 512], BF16); r_gT = S.pres("f_gT")
            stg = Rot([(sb("f_stg%d" % i, [128, 512], F32), S.pres("f_stg")) for i in range(3)])
            khb = Rot([(sb("f_khb%d" % i, [128, 1536], BF16), S.pres("f_khb")) for i in range(2)])
            vhb = Rot([(sb("f_vhb%d" % i, [128, 12, 128], BF16), S.pres("f_vhb")) for i in range(2)])
            PT = Rot([(sb("f_PT%d" % i, [128, 512], BF16), S.pres("f_PT")) for i in range(3)])
            rl = sb("f_rl", [128, 512], F32); r_rl = S.pres("f_rl")
            ctb = sb("f_ctb", [128, 4, 16], BF16); r_ctb = S.pres("f_ctb")
            S.op("dve", lambda e: e.tensor_scalar(ctb[:BS, 0:NB, :], self.negc[:BS, blk0:blk0 + NB, :], -(128.0 ** 0.5), None, ALU.mult),
                 reads=[self.r_negc], writes=[r_ctb])
            bank, r_b = self.rot.next()
            bb = bank.bitcast(BF16)
            for tb in range(NB):
                S.op("pe", lambda e: e.transpose(bb[:16, tb * BS:(tb + 1) * BS], ctb[:BS, tb, :], self.ident_b[:BS, :BS]),
                     reads=[r_ctb, self.r_cst], writes=[r_b])
            S.op("dve", lambda e: e.tensor_copy(self.cT_bf[:, 0:TT], bb[:16, 0:TT]), reads=[r_b], writes=[self.r_cT])

            for hg in range(4):
                w, r_w = self.wget("fq%d" % hg)
                for cc in range(4):
                    bank, r_b = self.proj_feat(w, r_w, cc * 128, hT, TT, [self.r_hT])
                    S.op("act", lambda e: e.copy(qT[:, cc, 0:TT], bank[:, 0:TT]), reads=[r_b], writes=[r_qT])
                w, r_w = self.wget("fk%d" % hg)
                for cc in range(4):
                    bank, r_b = self.proj_feat(w, r_w, cc * 128, hT, TT, [self.r_hT])
                    S.op("dve", lambda e: e.tensor_copy(kT[:, cc, 0:TT], bank[:, 0:TT]), reads=[r_b], writes=[r_kT])
                for tb in range(NB):
                    bank, r_b = self.proj_tok(hT, tb * BS, BS, w, r_w, 512, [self.r_hT])
                    sg, r_sg = stg.next()
                    S.op("act", lambda e: e.copy(sg[:BS, :], bank[:BS, :]), reads=[r_b], writes=[r_sg])
                    S.dma("act", fk_out[tb * BS:(tb + 1) * BS, hg * 512:(hg + 1) * 512], sg[:BS, :], reads=[r_sg])
                for cc in range(4):
                    h = hg * 4 + cc
                    S.dma("act", self.kT_hist[h, :, blk0 * 128:blk0 * 128 + TT], kT[:, cc, 0:TT], reads=[r_kT], writes=[self.r_khist])
                w, r_w = self.wget("fv%d" % hg)
                for tb in range(NB):
                    bank, r_b = self.proj_tok(hT, tb * BS, BS, w, r_w, 512, [self.r_hT])
                    sg, r_sg = stg.next()
                    S.op("act", lambda e: e.copy(sg[:BS, :], bank[:BS, :]), reads=[r_b], writes=[r_sg])
                    S.dma("act", fv_out[tb * BS:(tb + 1) * BS, hg * 512:(hg + 1) * 512], sg[:BS, :], reads=[r_sg])
                    S.op("dve", lambda e: e.tensor_copy(V_cur[:BS, tb, hg * 512:(hg + 1) * 512], bank[:BS, :]), reads=[r_b], writes=[r_Vcur])
                    if BS == 128:
                        S.dma("act", self.v_hist[hg * 4:(hg + 1) * 4, :, blk0 + tb, :].rearrange("h p d -> p h d"),
                              V_cur[:, tb, hg * 512:(hg + 1) * 512].rearrange("p (h d) -> p h d", h=4), reads=[r_Vcur], writes=[self.r_vhist])
                w, r_w = self.wget("fg%d" % hg)
                for cc in range(4):
                    bank, r_b = self.proj_feat(w, r_w, cc * 128, hT, TT, [self.r_hT])
                    S.op("act", lambda e: e.activation(gT[:, cc, 0:TT], bank[:, 0:TT], AF.Silu), reads=[r_b], writes=[r_gT])
                nk = nhist + NB
                items = [(cc, j) for cc in range(4) for j in range(nk)]
                hist = {}
                pend = {}

                def emit_scores(cc, j):
                    h = hg * 4 + cc
                    if j == 0 and nhist > 0:
                        kh, r_kh = khb.next()
                        vh, r_vh = vhb.next()
                        S.dma("sp", kh[:, 0:nhist * 128], self.kT_hist[h, :, 0:nhist * 128], reads=[self.r_khist], writes=[r_kh])
                        S.dma("sp", vh[:, 0:nhist, :], self.v_hist[h, :, 0:nhist, :], reads=[self.r_vhist], writes=[r_vh])
                        hist[cc] = (kh, r_kh, vh, r_vh)
                    if j < nhist:
                        kh, r_kh, vh, r_vh = hist[cc]
                        kb = 128
                        k_ap, r_k = kh[:, j * 128:(j + 1) * 128], r_kh
                        v_ap, r_v = vh[:, j, :], r_vh
                        q0 = 0
                        diag = False
                    else:
                        tb = j - nhist
                        kb = BS
                        k_ap, r_k = kT[:, cc, tb * BS:(tb + 1) * BS], r_kT
                        v_ap, r_v = V_cur[:BS, tb, h * 128:(h + 1) * 128], r_Vcur
                        q0 = tb * BS
                        diag = True
                    nq = TT - q0
                    bS, r_bS = self.rot.next()
                    S.mm(bS[:kb, 0:nq], k_ap, qT[:, cc, q0:TT], start=True, stop=False, reads=[r_k, r_qT], writes=[r_bS])
                    S.mm(bS[:kb, 0:nq], self.sel_b[:, h * 128:h * 128 + kb], self.cT_bf[:, q0:TT], start=False, stop=True,
                         reads=[self.r_cst, self.r_cT], writes=[r_bS])
                    pend[(cc, j)] = (bS, r_bS, kb, q0, nq, diag, v_ap, r_v)

                def emit_pv(cc, j):
                    h = hg * 4 + cc
                    bS, r_bS, kb, q0, nq, diag, v_ap, r_v = pend.pop((cc, j))
                    bO, r_bO = self.banks[4 + (cc % 2) * 2]
                    bL, r_bL = self.banks[5 + (cc % 2) * 2]
                    p_, r_p = PT.next()
                    S.op("act", lambda e: e.activation(p_[:kb, 0:nq], bS[:kb, 0:nq], AF.Exp, bias=self.negc[:kb, j, h:h + 1], scale=scale),
                         reads=[r_bS, self.r_negc], writes=[r_p])
                    if diag:
                        S.op("pool", lambda e: e.tensor_tensor(p_[:kb, 0:BS], p_[:kb, 0:BS], self.triU_b[:kb, 0:BS], ALU.mult),
                             reads=[r_p, self.r_cst], writes=[r_p])
                    S.mm(bO[:, q0:TT], v_ap, p_[:kb, 0:nq], start=(j == 0), stop=(j == nk - 1), reads=[r_v, r_p], writes=[r_bO])
                    S.mm(bL[:, q0:TT], self.ones_b[:kb, :], p_[:kb, 0:nq], start=(j == 0), stop=(j == nk - 1), reads=[self.r_cst, r_p], writes=[r_bL])
                    if j == nk - 1:
                        S.op("dve", lambda e: e.reciprocal(rl[:, 0:TT], bL[:, 0:TT]), reads=[r_bL], writes=[r_rl])
                        S.op("dve", lambda e: e.tensor_tensor(rl[:, 0:TT], rl[:, 0:TT], gT[:, cc, 0:TT], ALU.mult), reads=[r_rl, r_gT], writes=[r_rl])
                        S.op("dve", lambda e: e.tensor_tensor(yT[:, h, 0:TT], bO[:, 0:TT], rl[:, 0:TT], ALU.mult), reads=[r_bO, r_rl], writes=[r_yT])

                emit_scores(*items[0])
                for i, it in enumerate(items):
                    if i + 1 < len(items):
                        emit_scores(*items[i + 1])
                    emit_pv(*it)
            self.branch_project(ph, yT, r_yT, "wof", "gf", False, TT)
        S.phase_end()

    def tile_mem(self, TT, BS, NB, nxt=None):
        S = self.S
        hT = self.hT
        scale = 512.0 ** -0.5
        with ExitStack() as pst:
            sb = lambda name, shape, dt: self.sb(pst, name, shape, dt)
            ph = {"sig": Rot([(sb("m_sig%d" % i, [128, 512], F32), S.pres("m_sig")) for i in range(2)])}
            yT = sb("m_yT", [128, 16, 512], BF16); r_yT = S.pres("m_yT")
            memKT = sb("m_memKT", [128, 16, 256], BF16); r_memKT = S.pres("m_memKT")
            memV = sb("m_memV", [128, 2, 2048], BF16); r_memV = S.pres("m_memV")
            mqT = sb("m_mqT", [128, 4, 512], BF16); r_mqT = S.pres("m_mqT")
            mgT = sb("m_mgT", [128, 4, 512], BF16); r_mgT = S.pres("m_mgT")
            PT = [(sb("m_PT%d" % i, [128, 512], BF16), S.pres("m_PT")) for i in range(2)]
            rl = sb("m_rl", [128, 512], F32); r_rl = S.pres("m_rl")
            tmp = Rot([(sb("m_tmp%d" % i, [128, 512], F32), S.pres("m_tmp")) for i in range(2)])
            S.dma("sp", memKT[:], self.memKT_d, reads=[self.r_memKT_d], writes=[r_memKT])
            S.dma("sp", memV[:], self.memV_d, reads=[self.r_memV_d], writes=[r_memV])
            for mh in range(4):
                w, r_w = self.wget("mq%d" % mh)
                for cc in range(4):
                    bank, r_b = self.proj_feat(w, r_w, cc * 128, hT, TT, [self.r_hT])
                    S.op("act", lambda e: e.copy(mqT[:, cc, 0:TT], bank[:, 0:TT]), reads=[r_b], writes=[r_mqT])
                w, r_w = self.wget("mg%d" % mh)
                for cc in range(4):
                    bank, r_b = self.proj_feat(w, r_w, cc * 128, hT, TT, [self.r_hT])
                    S.op("act", lambda e: e.activation(mgT[:, cc, 0:TT], bank[:, 0:TT], AF.Silu), reads=[r_b], writes=[r_mgT])
                for mb in range(2):
                    bS, r_bS = self.rot.next()
                    for dc in range(4):
                        S.mm(bS[:, 0:TT], memKT[:, mh * 4 + dc, mb * 128:(mb + 1) * 128], mqT[:, dc, 0:TT], start=(dc == 0), stop=(dc == 3),
                             reads=[r_memKT, r_mqT], writes=[r_bS])
                    p_, r_p = PT[mb]
                    S.op("act", lambda e: e.activation(p_[:, 0:TT], bS[:, 0:TT], AF.Exp, scale=scale), reads=[r_bS], writes=[r_p])
                bL, r_bL = self.banks[4]
                for mb in range(2):
                    S.mm(bL[:, 0:TT], self.ones_b, PT[mb][0][:, 0:TT], start=(mb == 0), stop=(mb == 1), reads=[self.r_cst, PT[mb][1]], writes=[r_bL])
                S.op("dve", lambda e: e.reciprocal(rl[:, 0:TT], bL[:, 0:TT]), reads=[r_bL], writes=[r_rl])
                for dc in range(4):
                    bO, r_bO = self.banks[5 + (dc % 3)]
                    for mb in range(2):
                        S.mm(bO[:, 0:TT], memV[:, mb, (mh * 4 + dc) * 128:(mh * 4 + dc + 1) * 128], PT[mb][0][:, 0:TT], start=(mb == 0), stop=(mb == 1),
                             reads=[r_memV, PT[mb][1]], writes=[r_bO])
                    t_, r_t = tmp.next()
                    S.op("dve", lambda e: e.tensor_tensor(t_[:, 0:TT], rl[:, 0:TT], mgT[:, dc, 0:TT], ALU.mult), reads=[r_rl, r_mgT], writes=[r_t])
                    S.op("dve", lambda e: e.tensor_tensor(yT[:, mh * 4 + dc, 0:TT], bO[:, 0:TT], t_[:, 0:TT], ALU.mult), reads=[r_bO, r_t], writes=[r_yT])
            pre = []
            if nxt is not None:
                BSn = nxt["BS"]
                NBn = nxt["TT"] // BSn
                t1ph = {
                    "xs": Rot([(sb("m_t1xs%d" % i, [128, 2048], F32), S.pres("m_t1xs")) for i in range(2)]),
                    "junk": (sb("m_t1junk", [128, 2048], BF16), S.pres("m_t1junk")),
                    "small": Rot([(sb("m_t1sm%d" % i, [128, 4], F32), S.pres("m_t1sm")) for i in range(2)]),
                }
                for tb in range(min(2, NBn)):
                    pre.append(self.norm_load(t1ph, nxt["x_src"][tb * BSn:(tb + 1) * BSn, :], BSn, q="act"))
            self.branch_project(ph, yT, r_yT, "wom", "gm", False, TT)
            if nxt is not None:
                for tb, (xs_, r_xs_) in enumerate(pre):
                    self.norm_tr(xs_, r_xs_, BSn, self.hT, tb * BSn, self.gcols[:, 0:16], self.r_hT)
                for tb in range(2, NBn):
                    xs_, r_xs_ = self.norm_load(t1ph, nxt["x_src"][tb * BSn:(tb + 1) * BSn, :], BSn, q="act")
                    self.norm_tr(xs_, r_xs_, BSn, self.hT, tb * BSn, self.gcols[:, 0:16], self.r_hT)
        S.phase_end()

    def tile_final(self, TT, BS, NB, x_src, y_out):
        S = self.S
        with ExitStack() as pst:
            sb = lambda name, shape, dt: self.sb(pst, name, shape, dt)
            xs = sb("o_xs", [128, 4, 2048], F32); r_xs = [S.pres("o_xs%d" % i) for i in range(4)]
            xo = sb("o_xo", [128, 4, 2048], F32); r_xo = [S.pres("o_xo%d" % i) for i in range(4)]
            gfb = sb("o_gfb", [128, 2048], F32); r_gfb = S.pres("o_gfb")
            junk = sb("o_junk", [128, 2048], BF16); r_junk = S.pres("o_junk")
            st4 = [(sb("o_st%d" % i, [128, 4], F32), S.pres("o_st")) for i in range(4)]
            for tb in range(NB):
                S.dma("act", xs[:BS, tb, :], x_src[tb * BS:(tb + 1) * BS, :], writes=[r_xs[tb]])
            S.dma("act", gfb[:], self.I["g_final"].partition_broadcast(128), writes=[r_gfb])
            for j in range(4):
                w, r_w = self.wget("wout%d" % j)
                for tb in range(NB):
                    bank, r_b = self.proj_tok(self.mergedT, tb * BS, BS, w, r_w, 512, [self.r_merged])
                    S.op("dve", lambda e: e.tensor_tensor(xo[:BS, tb, j * 512:(j + 1) * 512], bank[:BS, :], xs[:BS, tb, j * 512:(j + 1) * 512], ALU.add),
                         reads=[r_b, r_xs[tb]], writes=[r_xo[tb]])
            for tb in range(NB):
                s4, r_s4 = st4[tb]
                S.op("act", lambda e: e.activation(junk[:BS, :], xo[:BS, tb, :], AF.Square, accum_out=s4[:BS, 0:1]), reads=[r_xo[tb]], writes=[r_junk, r_s4])
                S.op("act", lambda e: e.activation(s4[:BS, 1:2], s4[:BS, 0:1], AF.Sqrt, bias=self.cc[:BS, 0:1], scale=1.0 / D), reads=[r_s4, self.r_cst], writes=[r_s4])
                S.op("dve", lambda e: e.reciprocal(s4[:BS, 2:3], s4[:BS, 1:2]), reads=[r_s4], writes=[r_s4])
                S.op("dve", lambda e: e.scalar_tensor_tensor(xo[:BS, tb, :], xo[:BS, tb, :], s4[:BS, 2:3], gfb[:BS, :], ALU.mult, ALU.mult),
                     reads=[r_xo[tb], r_s4, r_gfb], writes=[r_xo[tb]])
                S.dma("act", y_out[tb * BS:(tb + 1) * BS, :], xo[:BS, tb, :], reads=[r_xo[tb]])
        S.phase_end()


_CACHE = {}


def _consts():
    ident = np.eye(128, dtype=np.float32)
    s = np.arange(128)
    triU = (s[:, None] <= s[None, :]).astype(np.float32)
    triLs = (s[:, None] > s[None, :]).astype(np.float32)
    ones = np.ones((128, 128), np.float32)
    cst = np.concatenate([ident, triU, triLs, ones], axis=1)
    sel = np.zeros((16, 16, 128), np.float32)
    for h in range(16):
        sel[h, h, :] = 1.0
    return cst, sel.reshape(16, 2048)


def kernel(x_prompt, x_sample, mem_prompt, cache_fox_k, cache_fox_v, cache_fox_logf, state_ssd, state_ssd_conv,
           cache_mem_k, cache_mem_v, g_norm, w_in, w_conv, b_conv, dt_bias, a_log, d_skip, g_ssd_out, b_forget,
           g_mem, w_mem_kv, w_o_ssd, w_o_fox, w_o_mem, w_out, g_final):
    f = lambda a: np.ascontiguousarray(np.asarray(a, dtype=np.float32))
    if "nc" not in _CACHE:
        _CACHE["nc"] = Prog().build()
    nc = _CACHE["nc"]
    cst, sel = _consts()
    gcols = np.concatenate([f(g_norm)[0].reshape(16, 128).T, f(g_ssd_out)[0].reshape(16, 128).T,
                            f(g_mem)[0].reshape(16, 128).T], axis=1)
    wconv_c = f(w_conv)[0].reshape(4, 24, 128).transpose(2, 1, 0).reshape(128, 96)
    bconv_c = f(b_conv)[0].reshape(24, 128).T
    hb = np.concatenate([np.broadcast_to(f(dt_bias)[0][None, :], (128, 32)), np.broadcast_to(f(a_log)[0][None, :], (128, 32)),
                         np.broadcast_to(f(d_skip)[0][None, :], (128, 32)), np.broadcast_to(f(b_forget)[0][None, :], (128, 16))], axis=1)
    shared = {
        "w_in": f(w_in)[0], "w_mem_kv": f(w_mem_kv)[0], "w_o_ssd": f(w_o_ssd)[0], "w_o_fox": f(w_o_fox)[0],
        "w_o_mem": f(w_o_mem)[0], "w_out": f(w_out)[0], "g_final": f(g_final),
        "gcols": f(gcols), "wconv_c": f(wconv_c), "bconv_c": f(bconv_c), "hb": f(hb), "cst": cst, "selc": sel,
    }
    xp = f(x_prompt); xs = f(x_sample); mp = f(mem_prompt)
    in_maps = []
    for c in range(8):
        m = dict(shared)
        m["xp"] = xp[2 * c:2 * c + 2]
        m["xs"] = xs[c]
        m["memp"] = mp[2 * c:2 * c + 2]
        m["ck"] = f(cache_fox_k)[0, c].reshape(1024, 2048)
        m["cv"] = f(cache_fox_v)[0, c].reshape(1024, 2048)
        m["clf"] = f(cache_fox_logf)[0, c]
        m["sst"] = f(state_ssd)[0, c].reshape(2048, 128)
        m["sconv"] = f(state_ssd_conv)[0, c]
        m["cmk"] = f(cache_mem_k)[0, c].reshape(256, 2048)
        m["cmv"] = f(cache_mem_v)[0, c].reshape(256, 2048)
        in_maps.append(m)
    res = run_bass_kernel_spmd(nc, in_maps, core_ids=list(range(8)))
    R = res.results
    cat = lambda k: np.concatenate([np.asarray(r[k]) for r in R], axis=0)
    stk = lambda k: np.stack([np.asarray(r[k]) for r in R], axis=0)
    y_prompt = cat("yp")
    y_sample = stk("ys")
    fkp = cat("fkp").reshape(1, 16, 2048, 16, 128)
    fvp = cat("fvp").reshape(1, 16, 2048, 16, 128)
    flp = cat("flp").reshape(1, 16, 2048, 16)
    ssdp = cat("ssdp").reshape(1, 16, 32, 64, 128)
    convp = cat("convp").reshape(1, 16, 3, 3072)
    mkp = cat("mkp").reshape(1, 16, 256, 4, 512)
    mvp = cat("mvp").reshape(1, 16, 256, 4, 512)
    fks = stk("fks").reshape(1, 8, 64, 16, 128)
    fvs = stk("fvs").reshape(1, 8, 64, 16, 128)
    fls = stk("fls").reshape(1, 8, 64, 16)
    ssds = stk("ssds").reshape(1, 8, 32, 64, 128)
    convs = stk("convs").reshape(1, 8, 3, 3072)
    return (y_prompt, y_sample, fkp, fvp, flp, ssdp, convp, mkp, mvp, fks, fvs, fls, ssds, convs)
```
